# Optimizing a Trainium2 kernel written in Bass

```python
import jax, jax.numpy as jnp
from jax import lax
import numpy as np

D_MODEL = 2048
BATCH = 2
SEQ = 16384
DEPTH = 2
DEC_BATCH = 16
DEC_SEQ = 2048
PAST_LEN = 128

FNET_GROUPS = 4
FNET_GROUP_DIM = 128
FNET_DIM = FNET_GROUPS * FNET_GROUP_DIM

MLA_HEADS = 6
Q_LORA = 512
KV_LORA = 512
QK_NOPE = 128
QK_ROPE = 64
V_HEAD = 128
QK_HEAD = QK_NOPE + QK_ROPE
MLA_DIM = MLA_HEADS * V_HEAD
ROPE_THETA = 10000.0
Q_BLOCK = 128

SSD_HEADS = 12
SSD_HEAD_DIM = 64
SSD_DIM = SSD_HEADS * SSD_HEAD_DIM
SSD_GROUPS = 2
SSD_HPG = SSD_HEADS // SSD_GROUPS
SSD_STATE = 128
CONV_K = 5
CHUNK = 128
CONV_DIM = SSD_DIM + 2 * SSD_GROUPS * SSD_STATE

MIX_DIM = FNET_DIM + MLA_DIM + SSD_DIM
D_FF = 4 * D_MODEL
EPS = 1e-6

IN_SPLITS = (FNET_DIM, Q_LORA, KV_LORA, QK_ROPE, SSD_DIM, CONV_DIM, SSD_HEADS, SSD_HEADS)
IN_DIM = FNET_DIM + Q_LORA + KV_LORA + QK_ROPE + SSD_DIM + CONV_DIM + 2 * SSD_HEADS

kernel_name = 'hybrid_fnet_mla_ssd_encoder'


def rmsnorm(x, g):
    xf = x.astype(jnp.float32)
    y = xf * lax.rsqrt(jnp.mean(jnp.square(xf), axis=-1, keepdims=True) + EPS)
    return (y * g.astype(jnp.float32)).astype(x.dtype)


def fourier_mix(u):
    b, s, _ = u.shape
    uf = u.astype(jnp.float32).reshape(b, s, FNET_GROUPS, FNET_GROUP_DIM)
    out = jnp.fft.fft2(uf, axes=(1, 3), norm='ortho').real
    return out.reshape(b, s, FNET_DIM).astype(u.dtype)


def rope_tables(s):
    pos = jnp.arange(s, dtype=jnp.float32)
    inv = ROPE_THETA ** (-jnp.arange(0, QK_ROPE, 2, dtype=jnp.float32) / QK_ROPE)
    ang = pos[:, None] * inv[None, :]
    return jnp.cos(ang), jnp.sin(ang)


def apply_rope(x, cos, sin):
    xf = x.astype(jnp.float32)
    x1, x2 = xf[..., : QK_ROPE // 2], xf[..., QK_ROPE // 2:]
    c, sn = cos[None, :, None, :], sin[None, :, None, :]
    return jnp.concatenate([x1 * c - x2 * sn, x2 * c + x1 * sn], axis=-1).astype(x.dtype)


def blocked_attention(q, k, v):
    b, s, h, dq = q.shape
    nb = s // Q_BLOCK
    scale = 1.0 / np.sqrt(QK_HEAD)
    qb = q.reshape(b, nb, Q_BLOCK, h, dq).transpose(1, 0, 2, 3, 4)

    def one_block(qi):
        sc = jnp.einsum('bqhd,bkhd->bhqk', qi, k).astype(jnp.float32) * scale
        p = jax.nn.softmax(sc, axis=-1).astype(v.dtype)
        return jnp.einsum('bhqk,bkhd->bqhd', p, v)

    out = lax.map(one_block, qb)
    return out.transpose(1, 0, 2, 3, 4).reshape(b, s, h * V_HEAD)


def mla_mix(c_q, c_kv, k_pe_raw, q_norm_g, w_q_up, kv_norm_g, w_kv_up):
    b, s, _ = c_q.shape
    q = (rmsnorm(c_q, q_norm_g) @ w_q_up).reshape(b, s, MLA_HEADS, QK_HEAD)
    kv = (rmsnorm(c_kv, kv_norm_g) @ w_kv_up).reshape(b, s, MLA_HEADS, QK_NOPE + V_HEAD)
    q_nope, q_pe = q[..., :QK_NOPE], q[..., QK_NOPE:]
    k_nope, v = kv[..., :QK_NOPE], kv[..., QK_NOPE:]
    cos, sin = rope_tables(s)
    q_pe = apply_rope(q_pe, cos, sin)
    k_pe = apply_rope(k_pe_raw[:, :, None, :], cos, sin)
    q = jnp.concatenate([q_nope, q_pe], axis=-1)
    k = jnp.concatenate([k_nope, jnp.broadcast_to(k_pe, (b, s, MLA_HEADS, QK_ROPE))], axis=-1)
    return blocked_attention(q, k, v)


def depthwise_conv(u, w, bias):
    out = lax.conv_general_dilated(
        u, w[:, None, :].astype(u.dtype), window_strides=(1,),
        padding=[(CONV_K // 2, CONV_K // 2)],
        dimension_numbers=('NWC', 'WIO', 'NWC'), feature_group_count=u.shape[-1])
    return out + bias.astype(u.dtype)


def segsum(a):
    t = a.shape[-1]
    cs = jnp.cumsum(a, axis=-1)
    diff = cs[..., :, None] - cs[..., None, :]
    mask = jnp.tril(jnp.ones((t, t), dtype=bool))
    return jnp.where(mask, diff, -jnp.inf)


def ssd_scan(x, dt, a, bm, cm):
    b, s, h, p = x.shape
    nc = s // CHUNK
    g, r, n = SSD_GROUPS, SSD_HPG, SSD_STATE
    xd = (x * dt[..., None]).reshape(b, nc, CHUNK, g, r, p)
    ad = (dt * a).reshape(b, nc, CHUNK, g, r).transpose(0, 1, 3, 4, 2)
    bm = bm.reshape(b, nc, CHUNK, g, n)
    cm = cm.reshape(b, nc, CHUNK, g, n)
    acum = jnp.cumsum(ad, axis=-1)
    lmat = jnp.exp(segsum(ad))
    cb = jnp.einsum('bclgn,bcsgn->bcgls', cm, bm)
    y_diag = jnp.einsum('bcgrls,bcsgrp->bclgrp', cb[:, :, :, None] * lmat, xd)
    decay_states = jnp.exp(acum[..., -1:] - acum)
    states = jnp.einsum('bclgn,bcgrl,bclgrp->bcgrpn', bm, decay_states, xd)
    chunk_decay = jnp.exp(acum[..., -1])

    def step(carry, inp):
        st, dec = inp
        return carry * dec[..., None, None] + st, carry

    init = jnp.zeros((b, g, r, p, n), jnp.float32)
    _, prev = lax.scan(step, init, (states.transpose(1, 0, 2, 3, 4, 5),
                                    chunk_decay.transpose(1, 0, 2, 3)))
    prev = prev.transpose(1, 0, 2, 3, 4, 5)
    y_off = jnp.einsum('bclgn,bcgrpn,bcgrl->bclgrp', cm, prev, jnp.exp(acum))
    return (y_diag + y_off).reshape(b, s, h, p)


def ssd_mix(z, xbc, dt_f_raw, dt_b_raw, conv_w, conv_b, dt_bias_f, dt_bias_b,
            a_log_f, a_log_b, d_skip, ssd_norm_g):
    b, s, _ = z.shape
    xbc = jax.nn.silu(depthwise_conv(xbc, conv_w, conv_b)).astype(jnp.float32)
    xs = xbc[..., :SSD_DIM].reshape(b, s, SSD_HEADS, SSD_HEAD_DIM)
    bm = xbc[..., SSD_DIM:SSD_DIM + SSD_GROUPS * SSD_STATE].reshape(b, s, SSD_GROUPS, SSD_STATE)
    cm = xbc[..., SSD_DIM + SSD_GROUPS * SSD_STATE:].reshape(b, s, SSD_GROUPS, SSD_STATE)
    dt_f = jax.nn.softplus(dt_f_raw.astype(jnp.float32) + dt_bias_f.astype(jnp.float32))
    dt_b = jax.nn.softplus(dt_b_raw.astype(jnp.float32) + dt_bias_b.astype(jnp.float32))
    a_f = -jnp.exp(a_log_f.astype(jnp.float32))
    a_b = -jnp.exp(a_log_b.astype(jnp.float32))
    flip = lambda t: jnp.flip(t, axis=1)
    y_f = ssd_scan(xs, dt_f, a_f, bm, cm)
    y_b = flip(ssd_scan(flip(xs), flip(dt_b), a_b, flip(bm), flip(cm)))
    y = y_f + y_b + xs * d_skip.astype(jnp.float32)[:, None]
    y = y.reshape(b, s, SSD_DIM) * jax.nn.silu(z.astype(jnp.float32))
    yg = y.reshape(b, s, SSD_GROUPS, SSD_DIM // SSD_GROUPS)
    yg = yg * lax.rsqrt(jnp.mean(jnp.square(yg), axis=-1, keepdims=True) + EPS)
    y = yg.reshape(b, s, SSD_DIM) * ssd_norm_g.astype(jnp.float32)
    return y.astype(z.dtype)


def encoder_layer(x, pre_mix_g, w_in, q_norm_g, w_q_up, kv_norm_g, w_kv_up, conv_w, conv_b,
                  dt_bias_f, dt_bias_b, a_log_f, a_log_b, d_skip, ssd_norm_g, w_out,
                  post_mix_g, pre_ffn_g, w_ff1, w_ff2, post_ffn_g):
    h = rmsnorm(x, pre_mix_g)
    proj = h @ w_in
    offs = np.cumsum(IN_SPLITS)[:-1].tolist()
    u_f, c_q, c_kv, k_pe, z, xbc, dtf, dtb = jnp.split(proj, offs, axis=-1)
    mix = jnp.concatenate([
        fourier_mix(u_f),
        mla_mix(c_q, c_kv, k_pe, q_norm_g, w_q_up, kv_norm_g, w_kv_up),
        ssd_mix(z, xbc, dtf, dtb, conv_w, conv_b, dt_bias_f, dt_bias_b,
                a_log_f, a_log_b, d_skip, ssd_norm_g),
    ], axis=-1) @ w_out
    x = x + rmsnorm(mix, post_mix_g)
    h = rmsnorm(x, pre_ffn_g)
    f = jnp.square(jax.nn.relu(h @ w_ff1)) @ w_ff2
    return x + rmsnorm(f, post_ffn_g)


def setup_inputs(seed: int = 0) -> dict:
    key = jax.random.key(seed)
    ks = jax.random.split(key, 24)
    f32 = jnp.float32
    nrm = lambda k, shape, scale: jax.random.normal(k, shape, f32) * scale
    gain = lambda k, dim: 1.0 + 0.02 * jax.random.normal(k, (DEPTH, dim), f32)
    dt0 = jnp.exp(jax.random.uniform(ks[12], (2, DEPTH, SSD_HEADS), f32,
                                     np.log(1e-3), np.log(1e-1)))
    dt_bias = dt0 + jnp.log(-jnp.expm1(-dt0))
    a_log = jnp.log(jax.random.uniform(ks[13], (2, DEPTH, SSD_HEADS), f32, 1.0, 16.0))
    return {
        'x_prompt': jax.random.normal(ks[0], (BATCH, SEQ, D_MODEL), f32),
        'x_sample': jax.random.normal(ks[1], (DEC_BATCH, DEC_SEQ, D_MODEL), f32),
        'pre_mix_g': gain(ks[2], D_MODEL),
        'w_in': nrm(ks[3], (DEPTH, D_MODEL, IN_DIM), D_MODEL ** -0.5),
        'q_norm_g': gain(ks[4], Q_LORA),
        'w_q_up': nrm(ks[5], (DEPTH, Q_LORA, MLA_HEADS * QK_HEAD), Q_LORA ** -0.5),
        'kv_norm_g': gain(ks[6], KV_LORA),
        'w_kv_up': nrm(ks[7], (DEPTH, KV_LORA, MLA_HEADS * (QK_NOPE + V_HEAD)), KV_LORA ** -0.5),
        'conv_w': nrm(ks[8], (DEPTH, CONV_K, CONV_DIM), CONV_K ** -0.5),
        'conv_b': nrm(ks[9], (DEPTH, CONV_DIM), 0.02),
        'dt_bias_f': dt_bias[0],
        'dt_bias_b': dt_bias[1],
        'a_log_f': a_log[0],
        'a_log_b': a_log[1],
        'd_skip': 1.0 + 0.1 * jax.random.normal(ks[10], (DEPTH, SSD_HEADS), f32),
        'ssd_norm_g': gain(ks[11], SSD_DIM),
        'w_out': nrm(ks[14], (DEPTH, MIX_DIM, D_MODEL), MIX_DIM ** -0.5),
        'post_mix_g': gain(ks[15], D_MODEL),
        'pre_ffn_g': gain(ks[16], D_MODEL),
        'w_ff1': nrm(ks[17], (DEPTH, D_MODEL, D_FF), D_MODEL ** -0.5),
        'w_ff2': nrm(ks[18], (DEPTH, D_FF, D_MODEL), D_FF ** -0.5),
        'post_ffn_g': gain(ks[19], D_MODEL),
    }


def reference(x_prompt, x_sample, pre_mix_g, w_in, q_norm_g, w_q_up, kv_norm_g, w_kv_up,
              conv_w, conv_b, dt_bias_f, dt_bias_b, a_log_f, a_log_b, d_skip, ssd_norm_g,
              w_out, post_mix_g, pre_ffn_g, w_ff1, w_ff2, post_ffn_g):
    weights = (pre_mix_g, w_in, q_norm_g, w_q_up, kv_norm_g, w_kv_up, conv_w, conv_b,
               dt_bias_f, dt_bias_b, a_log_f, a_log_b, d_skip, ssd_norm_g, w_out,
               post_mix_g, pre_ffn_g, w_ff1, w_ff2, post_ffn_g)

    def run_trunk(x):
        for l in range(DEPTH):
            x = encoder_layer(x, *[w[l] for w in weights])
        return x

    y_prompt = run_trunk(x_prompt)
    y_sample = run_trunk(x_sample)
    return (y_prompt, y_sample)
```

```python
from contextlib import ExitStack
import os
import numpy as np
import ml_dtypes
import concourse.bass as bass
import concourse.mybir as mybir
from concourse.bass_utils import run_bass_kernel_spmd

F32 = mybir.dt.float32
BF16 = mybir.dt.bfloat16
AF = mybir.ActivationFunctionType
ALU = mybir.AluOpType
AX = mybir.AxisListType
import os as _os
SAME_ENG_NOWAIT = _os.environ.get('SAME_ENG_NOWAIT', '0') == '1'


class Buf:
    def __init__(self, K, name, t):
        self.K = K
        self.name = name
        self.t = t
        self.w = None
        self.rs = {}
        self.dsem = None
        self.is_psum = False

    def __getitem__(self, k):
        return self.t[k]


class Eng:
    def __init__(self, K, name, eng, is_pe=False):
        self.K = K
        self.name = name
        self.eng = eng
        self.sem = K.new_sem("e_" + name)
        self.cnt = 0
        self.seen = {}
        self.is_pe = is_pe

    def wait(self, sem, val):
        if val <= 0:
            return
        key = id(sem)
        if self.seen.get(key, 0) < val:
            self.eng.wait_ge(sem, val)
            self.seen[key] = val

    def op(self, fn, reads=(), writes=(), inc=True):
        for b in reads:
            if b.w is not None:
                if not ((self.is_pe or SAME_ENG_NOWAIT) and b.w[0] is self.sem):
                    self.wait(*b.w)
            if b.is_psum:
                for s, v in b.rs.values():
                    if s is not self.sem:
                        self.wait(s, v)
        for b in writes:
            if b.w is not None:
                if not ((self.is_pe or SAME_ENG_NOWAIT) and b.w[0] is self.sem):
                    self.wait(*b.w)
            for s, v in b.rs.values():
                if s is not self.sem or not self.is_pe:
                    self.wait(s, v)
        ins = fn(self.eng)
        inc = True
        if inc:
            self.cnt += 1
            ins.then_inc(self.sem, 1)
            val = self.cnt
            self.K.latest[id(self.sem)] = (self.sem, val)
        else:
            val = self.cnt + 1
        for b in reads:
            b.rs[id(self.sem)] = (self.sem, val)
        for b in writes:
            b.w = (self.sem, val)
            b.rs = {}
        return ins

    def dma(self, out, in_, buf, load):
        if buf.dsem is None:
            buf.dsem = {}
        if self.name not in buf.dsem:
            buf.dsem[self.name] = self.K.get_dsem(self.name, buf.name)
        ent = buf.dsem[self.name]
        sem = ent[0]
        if buf.w is not None:
            self.wait(*buf.w)
        if load:
            for s, v in buf.rs.values():
                self.wait(s, v)
        ent[1] += 16
        tot = ent[1]
        self.eng.dma_start(out=out, in_=in_).then_inc(sem, 16)
        self.K.latest[id(sem)] = (sem, tot)
        if load:
            buf.w = (sem, tot)
            buf.rs = {}
        else:
            buf.rs[id(sem)] = (sem, tot)


class Kern:
    def __init__(self, nc, es):
        self.nc = nc
        self.es = es
        self.latest = {}
        self.nsem = 0
        self.sem_pool = {}
        self.phase_bufs = []
        self.pe = Eng(self, "pe", nc.tensor, is_pe=True)
        self.act = Eng(self, "act", nc.scalar)
        self.dve = Eng(self, "dve", nc.vector)
        self.pool = Eng(self, "pool", nc.gpsimd)
        self.sp = Eng(self, "sp", nc.sync)
        self.engs = [self.pe, self.act, self.dve, self.pool, self.sp]
        self.psum = []
        self.psum2 = []
        for i in range(4):
            big = self.es.enter_context(self.nc.psum_tensor("psb%d" % i, [128, 1024], F32))
            b2 = Buf(self, "psb%d" % i, big)
            b2.is_psum = True
            self.psum2.append(b2)
            for hh in range(2):
                b = Buf(self, "ps%d" % (2 * i + hh), big[:, hh * 512:(hh + 1) * 512])
                b.is_psum = True
                self.psum.append(b)
        self._flip = 0

    def new_sem(self, name):
        self.nsem += 1
        return self.es.enter_context(self.nc.semaphore(name + "_%d" % self.nsem))

    def get_dsem(self, engname, bufname):
        pool = self.sem_pool.setdefault(engname, [])
        if pool:
            return pool.pop()
        return [self.new_sem("d_" + engname), 0]

    def recycle(self):
        for b in self.phase_bufs:
            if b.dsem:
                for en, ent in b.dsem.items():
                    self.sem_pool.setdefault(en, []).append(ent)
                b.dsem = None
        self.phase_bufs = []

    def new_psum(self, name):
        t = self.es.enter_context(self.nc.psum_tensor(name, [128, 512], F32))
        b = Buf(self, name, t)
        b.is_psum = True
        return b

    def sb(self, es, name, shape, dtype):
        self.nsb = getattr(self, "nsb", 0) + 1
        name = "%s_%d" % (name, self.nsb)
        t = es.enter_context(self.nc.sbuf_tensor(name, list(shape), dtype))
        b = Buf(self, name, t)
        self.phase_bufs.append(b)
        return b

    def barrier(self, recycle=True):
        for e in self.engs:
            for s, v in list(self.latest.values()):
                e.wait(s, v)

    def ev(self):
        self._flip ^= 1
        return self.dve if self._flip else self.act


D = 2048
NKT = 16
TT = 512
IN_DIM = 3672
WIN_COLS = IN_DIM + 64
D_FF = 8192
EPS = 1e-6
QK_HEAD = 192
ROPE_THETA = 10000.0
C_U, C_CQ, C_CKV, C_KPE, C_Z, C_XBC, C_DT = 0, 512, 1024, 1536, 1600, 2368, 3648


def host_constants(seq_lens):
    bf = ml_dtypes.bfloat16
    c = {}
    c["ident_b"] = np.eye(128, dtype=np.float32).astype(bf)
    c["ident_f"] = np.eye(128, dtype=np.float32)
    c["onesD"] = np.full((128, 128), 1.0 / D, np.float32).astype(bf)
    c["ones512"] = np.full((128, 128), 1.0 / 512, np.float32).astype(bf)
    c["ones1"] = np.ones((128, 128), np.float32).astype(bf)
    c["ones_f"] = np.ones((128, 128), np.float32)
    smax = max(seq_lens)
    pos = np.arange(smax, dtype=np.float32)
    inv = (ROPE_THETA ** (-np.arange(0, 64, 2, dtype=np.float32) / 64)).astype(np.float32)
    ang = (pos[None, :] * inv[:, None]).astype(np.float32)
    c["cos128"] = np.tile(np.cos(ang), (4, 1)).astype(np.float32)
    c["sin128"] = np.tile(np.sin(ang), (4, 1)).astype(np.float32)
    t = np.arange(128)
    c["Uf"] = (t[:, None] > t[None, :]).astype(np.float32)
    c["Ub"] = (t[:, None] < t[None, :]).astype(np.float32)
    c["Tf"] = (t[:, None] <= t[None, :]).astype(np.float32)
    c["Tb"] = (t[:, None] >= t[None, :]).astype(np.float32)
    c["Mf"] = (t[None, :] >= t[:, None]).astype(np.float32)
    c["Mb"] = (t[None, :] <= t[:, None]).astype(np.float32)
    a = np.arange(128, dtype=np.float64)
    th = 2 * np.pi * np.outer(a, a) / 128.0
    c["C128"] = np.cos(th).astype(np.float32).astype(bf)
    c["S128"] = np.sin(th).astype(np.float32).astype(bf)
    c["nS128"] = (-np.sin(th)).astype(np.float32).astype(bf)
    for S in sorted(set(seq_lens)):
        na = S // 128
        aa = np.arange(na, dtype=np.float64)
        tha = 2 * np.pi * np.outer(aa, aa) / na
        cs = np.zeros((128, 2 * na), np.float32)
        cs[:na, :na] = np.cos(tha)
        cs[:na, na:] = np.sin(tha)
        c["dftA_%d" % S] = cs.astype(bf)
        b = np.arange(128, dtype=np.float64)
        tw = 2 * np.pi * np.outer(b, aa) / S
        c["twc_%d" % S] = np.cos(tw).astype(np.float32)
        c["tws_%d" % S] = np.sin(tw).astype(np.float32)
        sc = 1.0 / np.sqrt(S * 128.0)
        c["chC_%d" % S] = (np.cos(th) * sc).astype(np.float32).astype(bf)
        c["chnS_%d" % S] = (-np.sin(th) * sc).astype(np.float32).astype(bf)
    return c


class DR:
    pass


def build_program(seq_lens, depth, debug_outs=(), phases='paftswe'):
    nc = bass.Bass("TRN2", target_bir_lowering=False)
    T = sum(seq_lens)
    dr = DR()

    def din(name, shape, dt=F32):
        return nc.dram_tensor(name, list(shape), dt, kind="ExternalInput").ap()

    def dscr(name, shape, dt):
        kind = "ExternalOutput" if name in debug_outs else "Internal"
        return nc.dram_tensor(name, list(shape), dt, kind=kind).ap()

    dr.x = din("x", [T, D])
    dr.y = nc.dram_tensor("y", [T, D], F32, kind="ExternalOutput").ap()
    Ld = depth
    dr.pre_mix_g = din("pre_mix_g", [Ld, D]); dr.w_in = din("w_in", [Ld, D, IN_DIM])
    dr.q_norm_g = din("q_norm_g", [Ld, 512]); dr.w_q_up = din("w_q_up", [Ld, 512, 1152])
    dr.kv_norm_g = din("kv_norm_g", [Ld, 512]); dr.w_kv_up = din("w_kv_up", [Ld, 512, 1536])
    dr.conv_w = din("conv_w", [Ld, 5, 1280]); dr.conv_b = din("conv_b", [Ld, 1280])
    dr.dt_bias = din("dt_bias", [Ld, 24]); dr.a_log = din("a_log", [Ld, 24])
    dr.d_skip = din("d_skip", [Ld, 12]); dr.ssd_norm_g = din("ssd_norm_g", [Ld, 768])
    dr.w_out = din("w_out", [Ld, D, D]); dr.post_mix_g = din("post_mix_g", [Ld, D])
    dr.pre_ffn_g = din("pre_ffn_g", [Ld, D]); dr.w_ff1 = din("w_ff1", [Ld, D, D_FF])
    dr.w_ff2 = din("w_ff2", [Ld, D_FF, D]); dr.post_ffn_g = din("post_ffn_g", [Ld, D])
    consts = host_constants(seq_lens)
    dr.c = {}
    for k, v in consts.items():
        dr.c[k] = din("c_" + k, v.shape, BF16 if v.dtype == ml_dtypes.bfloat16 else F32)
    dr.xT = dscr("xT", [D, T], F32)
    dr.w_in_b = dscr("w_in_b", [Ld, D, WIN_COLS], BF16)
    dr.w_q_b = dscr("w_q_b", [Ld, 512, 1536], BF16)
    dr.w_kv_b = dscr("w_kv_b", [Ld, 512, 1536], BF16)
    dr.w_out_b = dscr("w_out_b", [Ld, 4, 128, NKT, 512], BF16)
    dr.w_ff1_b = dscr("w_ff1_b", [Ld, 16, 128, NKT, 512], BF16)
    dr.w_ff2_b = dscr("w_ff2_b", [Ld, 16, 128, NKT, 512], BF16)
    dr.u_d = dscr("u_d", [4, T, 128], BF16)
    dr.qT = dscr("qT", [6, 192, T], BF16)
    dr.kT = dscr("kT", [6, 128, T], BF16)
    dr.kpeT = dscr("kpeT", [64, T], BF16)
    dr.v_tok = dscr("v_tok", [T, 768], BF16)
    dr.xbcT = dscr("xbcT", [1280, T], BF16)
    dr.z_tok = dscr("z_tok", [T, 768], BF16)
    dr.dt_tok = dscr("dt_tok", [T, 64], F32)
    dr.mixT = dscr("mixT", [D, T], BF16)
    dr.pT = dscr("pT", [512, T], BF16)
    dr.qqT = dscr("qqT", [512, T], BF16)
    dr.yf = dscr("yf", [T, 768], F32)

    with ExitStack() as es:
        K = Kern(nc, es)
        K.dr = dr
        K.seqs = []
        t0 = 0
        for S in seq_lens:
            K.seqs.append((t0, S))
            t0 += S
        K.T = T
        if 'p' in phases:
            prologue(K, depth)
            K.barrier()
            K.recycle()
        for L in range(depth):
            for ch, fn in (('a', phase1), ('f', phase_fnet), ('t', phase_attn), ('s', phase_ssd), ('w', phase_w)):
                if ch in phases:
                    fn(K, L)
                    K.barrier()
                    K.recycle()
        if 'e' in phases:
            epilogue(K)
            K.barrier()
    return nc, consts


def dma_dd(K, out, in_):
    if not hasattr(K, "dd_sem"):
        K.dd_sem = K.new_sem("dd")
        K.dd_tot = 0
    K.dd_tot += 16
    K.pool.eng.dma_start(out=out, in_=in_).then_inc(K.dd_sem, 16)
    K.latest[id(K.dd_sem)] = (K.dd_sem, K.dd_tot)


def prologue(K, depth):
    import os
    parts = os.environ.get("PRO_PARTS", "123")
    dr = K.dr
    nc = K.nc
    for L in range(depth if '1' in parts else 0):
        for r in range(0, D, 256):
            dma_dd(K, dr.w_in_b[L, r:r + 256, 0:IN_DIM], dr.w_in[L, r:r + 256, :])
        for cg in range(4):
            dma_dd(K, dr.w_out_b[L, cg], dr.w_out[L, :, cg * 512:(cg + 1) * 512].rearrange("(kt p) c -> p kt c", p=128))
        for cg in range(16):
            dma_dd(K, dr.w_ff1_b[L, cg], dr.w_ff1[L, :, cg * 512:(cg + 1) * 512].rearrange("(kt p) c -> p kt c", p=128))
        for cg in range(4):
            for kb in range(4):
                dma_dd(K, dr.w_ff2_b[L, cg * 4 + kb],
                       dr.w_ff2[L, kb * 2048:(kb + 1) * 2048, cg * 512:(cg + 1) * 512].rearrange("(kt p) c -> p kt c", p=128))
        src = dr.w_kv_up[L].rearrange("k (h c) -> k h c", h=6)
        dma_dd(K, dr.w_kv_b[L, :, 0:768].rearrange("k (h c) -> k h c", h=6), src[:, :, 0:128])
        dma_dd(K, dr.w_kv_b[L, :, 768:1536].rearrange("k (h c) -> k h c", h=6), src[:, :, 128:256])
    with ExitStack() as es:
        wq = K.sb(es, "p_wq", [128, 4, 1152], F32)
        wqo = K.sb(es, "p_wqo", [128, 4, 1536], BF16)
        kp = K.sb(es, "p_kp", [128, 16, 64], F32)
        kpo = K.sb(es, "p_kpo", [128, 16, 64], BF16)
        sc = 1.0 / np.sqrt(QK_HEAD)
        for L in range(depth if '2' in parts else 0):
            K.sp.dma(wq[:], dr.w_q_up[L].rearrange("(kt p) c -> p kt c", p=128), wq, True)
            wq4 = wq[:].rearrange("p k (h c) -> p k h c", h=6)
            K.dve.op(lambda e: e.tensor_scalar_mul(wqo[:, :, 0:768].rearrange("p k (h c) -> p k h c", h=6),
                                                   wq4[:, :, :, 0:128], sc), reads=[wq], writes=[wqo])
            K.dve.op(lambda e: e.tensor_scalar_mul(wqo[:, :, 768:1152].rearrange("p k (h c) -> p k h c", h=6),
                                                   wq4[:, :, :, 128:192], sc), reads=[wq], writes=[wqo])
            rot = wqo[:, :, 1152:1536].rearrange("p k (h c) -> p k h c", h=6)
            K.dve.op(lambda e: e.tensor_scalar_mul(rot[:, :, :, 0:32], wq4[:, :, :, 160:192], -sc), reads=[wq], writes=[wqo])
            K.dve.op(lambda e: e.tensor_scalar_mul(rot[:, :, :, 32:64], wq4[:, :, :, 128:160], sc), reads=[wq], writes=[wqo])
            K.pool.dma(dr.w_q_b[L].rearrange("(kt p) c -> p kt c", p=128), wqo[:], wqo, False)
            K.sp.dma(kp[:], dr.w_in[L, :, C_KPE:C_KPE + 64].rearrange("(kt p) c -> p kt c", p=128), kp, True)
            K.dve.op(lambda e: e.tensor_scalar_mul(kpo[:, :, 0:32], kp[:, :, 32:64], -1.0), reads=[kp], writes=[kpo])
            K.dve.op(lambda e: e.tensor_copy(kpo[:, :, 32:64], kp[:, :, 0:32]), reads=[kp], writes=[kpo])
            K.pool.dma(dr.w_in_b[L, :, IN_DIM:WIN_COLS].rearrange("(kt p) c -> p kt c", p=128), kpo[:], kpo, False)
        K.barrier()
    with ExitStack() as es:
        idf = K.sb(es, "p_idf", [128, 128], F32)
        K.sp.dma(idf[:], dr.c["ident_f"], idf, True)
        xin = [K.sb(es, "p_xin%d" % i, [128, D], F32) for i in range(2)]
        xo = [K.sb(es, "p_xo%d" % i, [128, NKT, TT], F32) for i in range(2)]
        nt = K.T // TT
        pi = 0
        for ti in range(nt if '3' in parts else 0):
            o = xo[ti % 2]
            for s4 in range(4):
                r0 = ti * TT + s4 * 128
                xi = xin[(ti * 4 + s4) % 2]
                K.sp.dma(xi[:], dr.x[r0:r0 + 128, :], xi, True)
                for kq in range(4):
                    ps = K.psum[pi % 8]
                    pi += 1
                    for k4 in range(4):
                        kt = kq * 4 + k4
                        K.pe.op(lambda e: e.transpose(ps[:, k4 * 128:(k4 + 1) * 128], xi[:, kt * 128:(kt + 1) * 128], idf[:]),
                                reads=[xi, idf], writes=[ps], inc=(k4 == 3))
                    ev = K.ev()
                    src = ps[:, :].rearrange("p (k t) -> p k t", k=4)
                    dst = o[:, kq * 4:(kq + 1) * 4, s4 * 128:(s4 + 1) * 128]
                    if ev is K.dve:
                        ev.op(lambda e: e.tensor_copy(dst, src), reads=[ps], writes=[o])
                    else:
                        ev.op(lambda e: e.activation(dst, src, AF.Copy), reads=[ps], writes=[o])
            K.pool.dma(dr.xT[:, ti * TT:(ti + 1) * TT].rearrange("(kt p) t -> p kt t", p=128), o[:], o, False)


def epilogue(K):
    dr = K.dr
    with ExitStack() as es:
        idf = K.sb(es, "e_idf", [128, 128], F32)
        K.sp.dma(idf[:], dr.c["ident_f"], idf, True)
        xin = [K.sb(es, "e_xin%d" % i, [128, NKT, TT], F32) for i in range(2)]
        yo = [K.sb(es, "e_yo%d" % i, [128, D], F32) for i in range(2)]
        nt = K.T // TT
        pi = 0
        for ti in range(nt):
            xi = xin[ti % 2]
            K.sp.dma(xi[:], dr.xT[:, ti * TT:(ti + 1) * TT].rearrange("(kt p) t -> p kt t", p=128), xi, True)
            for s4 in range(4):
                o = yo[(ti * 4 + s4) % 2]
                for kq in range(4):
                    ps = K.psum[pi % 8]
                    pi += 1
                    for k4 in range(4):
                        kt = kq * 4 + k4
                        K.pe.op(lambda e: e.transpose(ps[:, k4 * 128:(k4 + 1) * 128], xi[:, kt, s4 * 128:(s4 + 1) * 128], idf[:]),
                                reads=[xi, idf], writes=[ps], inc=(k4 == 3))
                    ev = K.ev()
                    dst = o[:, kq * 512:(kq + 1) * 512]
                    if ev is K.dve:
                        ev.op(lambda e: e.tensor_copy(dst, ps[:, :]), reads=[ps], writes=[o])
                    else:
                        ev.op(lambda e: e.activation(dst, ps[:, :], AF.Copy), reads=[ps], writes=[o])
                r0 = ti * TT + s4 * 128
                K.pool.dma(dr.y[r0:r0 + 128, :], o[:], o, False)


def evac(K, dst, dst_buf, ps, src, scale_ap=None, extra_reads=()):
    ev = K.ev()
    rd = [ps] + list(extra_reads)
    if ev is K.dve:
        if scale_ap is None:
            ev.op(lambda e: e.tensor_copy(dst, src), reads=rd, writes=[dst_buf])
        else:
            ev.op(lambda e: e.tensor_scalar_mul(dst, src, scale_ap), reads=rd, writes=[dst_buf])
    else:
        if scale_ap is None:
            ev.op(lambda e: e.activation(dst, src, AF.Copy), reads=rd, writes=[dst_buf])
        else:
            ev.op(lambda e: e.activation(dst, src, AF.Copy, scale=scale_ap), reads=rd, writes=[dst_buf])


class Rot:
    def __init__(self, items):
        self.items = items
        self.i = 0

    def next(self):
        x = self.items[self.i % len(self.items)]
        self.i += 1
        return x


def rms_rep(K, ps, src_fn, nk, ones, sqs, src_bufs, rstd, n=TT):
    for kt in range(nk):
        sq = sqs.next()
        K.act.op(lambda e: e.activation(sq[:, 0:n], src_fn(kt), AF.Square), reads=src_bufs, writes=[sq])
        K.pe.op(lambda e: e.matmul(ps[:, 0:n], ones[:], sq[:, 0:n], start=(kt == 0), stop=(kt == nk - 1)),
                reads=[sq, ones], writes=[ps], inc=(kt == nk - 1))
    K.act.op(lambda e: e.activation(rstd[:, 0:n], ps[:, 0:n], AF.Sqrt, bias=EPS, scale=1.0), reads=[ps], writes=[rstd])
    K.dve.op(lambda e: e.reciprocal(rstd[:, 0:n], rstd[:, 0:n]), reads=[rstd], writes=[rstd])


def phase1(K, L):
    import os
    P1_STOP = int(os.environ.get('P1_STOP', '99'))
    _only = os.environ.get('P1_ONLY', '')
    P1_ON = lambda n: (str(n) in _only) if _only else (n <= P1_STOP)
    dr = K.dr
    with ExitStack() as es:
        xt = K.sb(es, "a_xt", [128, NKT, TT], F32)
        hT = K.sb(es, "a_hT", [128, NKT, TT], BF16)
        sqs = Rot([K.sb(es, "a_sq%d" % i, [128, TT], BF16) for i in range(3)])
        wbs = Rot([K.sb(es, "a_wb%d" % i, [128, NKT, 512], BF16) for i in range(2)])
        rstd = K.sb(es, "a_rstd", [128, TT], F32)
        rq = K.sb(es, "a_rq", [128, TT], F32)
        rkv = K.sb(es, "a_rkv", [128, TT], F32)
        cq = K.sb(es, "a_cq", [128, 4, TT], F32)
        ckv = K.sb(es, "a_ckv", [128, 4, TT], F32)
        cqn = K.sb(es, "a_cqn", [128, 4, TT], BF16)
        ckvn = K.sb(es, "a_ckvn", [128, 4, TT], BF16)
        wq = K.sb(es, "a_wq", [128, 4, 1536], BF16)
        wkv = K.sb(es, "a_wkv", [128, 4, 1536], BF16)
        cosb = K.sb(es, "a_cos", [128, TT], F32)
        sinb = K.sb(es, "a_sin", [128, TT], F32)
        onesD = K.sb(es, "a_onesD", [128, 128], BF16)
        ones5 = K.sb(es, "a_ones5", [128, 128], BF16)
        gpre = K.sb(es, "a_gpre", [128, NKT], F32)
        gq = K.sb(es, "a_gq", [128, 4], F32)
        gkv = K.sb(es, "a_gkv", [128, 4], F32)
        stg = Rot([K.sb(es, "a_stg%d" % i, [128, 512], BF16) for i in range(int(os.environ.get("NSTG", "6")))])
        stz = Rot([K.sb(es, "a_stz%d" % i, [128, 768], BF16) for i in range(2)])
        stf = Rot([K.sb(es, "a_stf%d" % i, [128, 64], F32) for i in range(2)])
        t1 = Rot([K.sb(es, "a_t1%d" % i, [128, TT], F32) for i in range(2)])
        t2 = Rot([K.sb(es, "a_t2%d" % i, [128, TT], F32) for i in range(2)])
        psr = Rot(K.psum[2:8])
        ps_st = K.psum[0]
        ps_st2 = K.psum[1]

        K.sp.dma(onesD[:], dr.c["onesD"], onesD, True)
        K.sp.dma(ones5[:], dr.c["ones512"], ones5, True)
        with K.nc.allow_non_contiguous_dma("small gain vectors"):
            K.sp.dma(gpre[:], dr.pre_mix_g[L].rearrange("(kt p) -> p kt", p=128), gpre, True)
            K.sp.dma(gq[:], dr.q_norm_g[L].rearrange("(kt p) -> p kt", p=128), gq, True)
            K.sp.dma(gkv[:], dr.kv_norm_g[L].rearrange("(kt p) -> p kt", p=128), gkv, True)
        K.sp.dma(wq[:], dr.w_q_b[L].rearrange("(kt p) c -> p kt c", p=128), wq, True)
        K.sp.dma(wkv[:], dr.w_kv_b[L].rearrange("(kt p) c -> p kt c", p=128), wkv, True)
        wsrc = dr.w_in_b[L].rearrange("(kt p) c -> p kt c", p=128)

        def load_w(c0, n, dst0=0, wb=None):
            if wb is None:
                wb = wbs.next()
            K.sp.dma(wb[:, :, dst0:dst0 + n], wsrc[:, :, c0:c0 + n], wb, True)
            return wb

        def feat_block(wb, col0, M, rhs_buf, rhs_fn, nk=NKT):
            ps = psr.next()
            for kt in range(nk):
                K.pe.op(lambda e: e.matmul(ps[0:M, :], wb[:, kt, col0:col0 + M], rhs_fn(kt), start=(kt == 0), stop=(kt == nk - 1)),
                        reads=[wb, rhs_buf], writes=[ps], inc=(kt == nk - 1))
            return ps

        def tok_block(wb, col0, n, ts, lhs_buf, lhs_fn, nk=NKT):
            ps = psr.next()
            for kt in range(nk):
                K.pe.op(lambda e: e.matmul(ps[:, 0:n], lhs_fn(kt, ts), wb[:, kt, col0:col0 + n], start=(kt == 0), stop=(kt == nk - 1)),
                        reads=[wb, lhs_buf], writes=[ps], inc=(kt == nk - 1))
            return ps

        h_rhs = lambda kt: hT[:, kt, :]
        h_lhs = lambda kt, ts: hT[:, kt, ts * 128:(ts + 1) * 128]

        for (T0, S) in K.seqs:
            for j in range(S // TT):
                ta = T0 + j * TT
                K.sp.dma(xt[:], dr.xT[:, ta:ta + TT].rearrange("(kt p) t -> p kt t", p=128), xt, True)
                K.sp.dma(cosb[:], dr.c["cos128"][:, j * TT:(j + 1) * TT], cosb, True)
                K.sp.dma(sinb[:], dr.c["sin128"][:, j * TT:(j + 1) * TT], sinb, True)
                rms_rep(K, ps_st, lambda kt: xt[:, kt, :], NKT, onesD, sqs, [xt], rstd)
                for kt in range(NKT):
                    K.dve.op(lambda e: e.scalar_tensor_tensor(hT[:, kt, :], xt[:, kt, :], gpre[:, kt:kt + 1], rstd[:], op0=ALU.mult, op1=ALU.mult),
                             reads=[xt, gpre, rstd], writes=[hT])
                if P1_ON(1):
                    wb = load_w(C_CQ, 512)
                    for m in range(4):
                        ps = feat_block(wb, m * 128, 128, hT, h_rhs)
                        evac(K, cq[:, m, :], cq, ps, ps[:, :])
                    wb = load_w(C_CKV, 512)
                    for m in range(4):
                        ps = feat_block(wb, m * 128, 128, hT, h_rhs)
                        evac(K, ckv[:, m, :], ckv, ps, ps[:, :])
                if P1_ON(2):
                    rms_rep(K, ps_st2, lambda kt: cq[:, kt, :], 4, ones5, sqs, [cq], rq)
                    rms_rep(K, ps_st, lambda kt: ckv[:, kt, :], 4, ones5, sqs, [ckv], rkv)
                    for kt in range(4):
                        K.dve.op(lambda e: e.scalar_tensor_tensor(cqn[:, kt, :], cq[:, kt, :], gq[:, kt:kt + 1], rq[:], op0=ALU.mult, op1=ALU.mult),
                                 reads=[cq, gq, rq], writes=[cqn])
                        K.dve.op(lambda e: e.scalar_tensor_tensor(ckvn[:, kt, :], ckv[:, kt, :], gkv[:, kt:kt + 1], rkv[:], op0=ALU.mult, op1=ALU.mult),
                                 reads=[ckv, gkv, rkv], writes=[ckvn])
                if P1_ON(3):
                    wb = load_w(C_KPE, 64)
                    load_w(IN_DIM, 64, dst0=64, wb=wb)
                    psa = feat_block(wb, 0, 64, hT, h_rhs)
                    psb = feat_block(wb, 64, 64, hT, h_rhs)
                    a1 = t1.next(); a2 = t2.next(); so = stg.next()
                    K.dve.op(lambda e: e.tensor_tensor(a1[0:64, :], psa[0:64, :], cosb[0:64, :], op=ALU.mult), reads=[psa, cosb], writes=[a1])
                    K.dve.op(lambda e: e.tensor_tensor(a2[0:64, :], psb[0:64, :], sinb[0:64, :], op=ALU.mult), reads=[psb, sinb], writes=[a2])
                    K.dve.op(lambda e: e.tensor_tensor(so[0:64, :], a1[0:64, :], a2[0:64, :], op=ALU.add), reads=[a1, a2], writes=[so])
                    K.pool.dma(dr.kpeT[:, ta:ta + TT], so[0:64, :], so, False)
                if P1_ON(4):
                    c_rhs = lambda kt: cqn[:, kt, :]
                    for h in range(6):
                        ps = feat_block(wq, h * 128, 128, cqn, c_rhs, nk=4)
                        so = stg.next()
                        evac(K, so[:, :], so, ps, ps[:, :])
                        K.pool.dma(dr.qT[h, 0:128, ta:ta + TT], so[:, :], so, False)
                    for hp in range(3):
                        psa = feat_block(wq, 768 + hp * 128, 128, cqn, c_rhs, nk=4)
                        psb = feat_block(wq, 1152 + hp * 128, 128, cqn, c_rhs, nk=4)
                        a1 = t1.next(); a2 = t2.next(); so = stg.next()
                        K.dve.op(lambda e: e.tensor_tensor(a1[:, :], psa[:, :], cosb[:, :], op=ALU.mult), reads=[psa, cosb], writes=[a1])
                        K.dve.op(lambda e: e.tensor_tensor(a2[:, :], psb[:, :], sinb[:, :], op=ALU.mult), reads=[psb, sinb], writes=[a2])
                        K.dve.op(lambda e: e.tensor_tensor(so[:, :], a1[:, :], a2[:, :], op=ALU.add), reads=[a1, a2], writes=[so])
                        K.pool.dma(dr.qT[2 * hp, 128:192, ta:ta + TT], so[0:64, :], so, False)
                        K.pool.dma(dr.qT[2 * hp + 1, 128:192, ta:ta + TT], so[64:128, :], so, False)
                if P1_ON(5):
                    kv_rhs = lambda kt: ckvn[:, kt, :]
                    for h in range(6):
                        ps = feat_block(wkv, h * 128, 128, ckvn, kv_rhs, nk=4)
                        so = stg.next()
                        evac(K, so[:, :], so, ps, ps[:, :])
                        K.pool.dma(dr.kT[h, :, ta:ta + TT], so[:, :], so, False)
                    kv_lhs = lambda kt, ts: ckvn[:, kt, ts * 128:(ts + 1) * 128]
                    for ts in range(4):
                        sz = stz.next()
                        ps = tok_block(wkv, 768, 512, ts, ckvn, kv_lhs, nk=4)
                        evac(K, sz[:, 0:512], sz, ps, ps[:, :])
                        ps = tok_block(wkv, 768 + 512, 256, ts, ckvn, kv_lhs, nk=4)
                        evac(K, sz[:, 512:768], sz, ps, ps[:, 0:256])
                        K.pool.dma(dr.v_tok[ta + ts * 128:ta + (ts + 1) * 128, :], sz[:, :], sz, False)
                if P1_ON(6):
                    wb = load_w(C_U, 512)
                    for ts in range(4):
                        ps = tok_block(wb, 0, 512, ts, hT, h_lhs)
                        so = stg.next()
                        evac(K, so[:, :], so, ps, ps[:, :])
                        r0 = ta + ts * 128
                        K.pool.dma(dr.u_d[:, r0:r0 + 128, :].rearrange("g t c -> t g c"), so[:, :].rearrange("p (g c) -> p g c", g=4), so, False)
                if P1_ON(7):
                    P1_VAR = int(os.environ.get('P1_VAR', '0'))
                    wb = load_w(C_Z, 512)
                    wb2 = load_w(C_Z + 512, 256)
                    if P1_VAR not in (1, 5):
                        load_w(C_DT, 24, dst0=256, wb=wb2)
                        K.dve.op(lambda e: e.memset(wb2[:, :, 280:320], 0.0), writes=[wb2])
                    for ts in range(4):
                        sz = stz.next()
                        ps = tok_block(wb, 0, 512, ts, hT, h_lhs)
                        evac(K, sz[:, 0:512], sz, ps, ps[:, :])
                        ps = tok_block(wb2, 0, 256 if P1_VAR == 1 else 320, ts, hT, h_lhs)
                        if P1_VAR == 6:
                            K.dve.op(lambda e: e.tensor_copy(sz[:, 512:768], ps[:, 0:256]), reads=[ps], writes=[sz])
                        else:
                            evac(K, sz[:, 512:768], sz, ps, ps[:, 0:256])
                        r0 = ta + ts * 128
                        K.pool.dma(dr.z_tok[r0:r0 + 128, :], sz[:, :], sz, False)
                        if P1_VAR == 1:
                            continue
                        sf = stf.next()
                        K.dve.op(lambda e: e.tensor_copy(sf[:, :], ps[:, 256:320]), reads=[ps], writes=[sf])
                        if P1_VAR != 2:
                            if P1_VAR == 3:
                                K.pool.dma(dr.yf[r0:r0 + 128, 0:64], sf[:, :], sf, False)
                            elif P1_VAR == 4:
                                K.pool.dma(dr.dt_tok[r0:r0 + 128, :], t1.items[0][:, 0:64], t1.items[0], False)
                            else:
                                K.pool.dma(dr.dt_tok[r0:r0 + 128, :], sf[:, :], sf, False)
                if P1_ON(8):
                    for (c0, n) in ((C_XBC, 512), (C_XBC + 512, 512), (C_XBC + 1024, 256)):
                        wb = load_w(c0, n)
                        for m in range(n // 128):
                            ps = feat_block(wb, m * 128, 128, hT, h_rhs)
                            so = stg.next()
                            evac(K, so[:, :], so, ps, ps[:, :])
                            f0 = (c0 - C_XBC) + m * 128
                            K.pool.dma(dr.xbcT[f0:f0 + 128, ta:ta + TT], so[:, :], so, False)


def phase_fnet(K, L):
    dr = K.dr
    with ExitStack() as es:
        xg = K.sb(es, "f_xg", [128, 128, 128], BF16)
        bre = K.sb(es, "f_bre", [128, 128 * 128], BF16)
        bim = K.sb(es, "f_bim", [128, 128 * 128], BF16)
        pst = Rot([K.sb(es, "f_pst%d" % i, [128, 2048], BF16) for i in range(2)])
        qst = Rot([K.sb(es, "f_qst%d" % i, [128, 2048], BF16) for i in range(2)])
        dft = K.sb(es, "f_dft", [128, 256], BF16)
        twc = K.sb(es, "f_twc", [128, 128], F32)
        tws = K.sb(es, "f_tws", [128, 128], F32)
        c128 = K.sb(es, "f_c128", [128, 128], BF16)
        s128 = K.sb(es, "f_s128", [128, 128], BF16)
        ns128 = K.sb(es, "f_ns128", [128, 128], BF16)
        tt = [Rot([K.sb(es, "f_t%d_%d" % (q, i), [128, 512], F32) for i in range(2)]) for q in range(4)]
        K.sp.dma(c128[:], dr.c["C128"], c128, True)
        K.sp.dma(s128[:], dr.c["S128"], s128, True)
        K.sp.dma(ns128[:], dr.c["nS128"], ns128, True)
        psr = Rot(K.psum[0:4])
        psP = Rot(K.psum[4:6])
        psQ = Rot(K.psum[6:8])
        for (T0, S) in K.seqs:
            na = S // 128
            nch = 512 // (2 * na)
            if nch > 128:
                nch = 128
            K.sp.dma(dft[:, 0:2 * na], dr.c["dftA_%d" % S], dft, True)
            K.sp.dma(twc[:, 0:na], dr.c["twc_%d" % S], twc, True)
            K.sp.dma(tws[:, 0:na], dr.c["tws_%d" % S], tws, True)
            for g in range(4):
                K.sp.dma(xg[0:na, :, :], dr.u_d[g, T0:T0 + S, :].rearrange("(a b) c -> a b c", b=128), xg, True)
                bre3 = bre[:, 0:128 * na].rearrange("p (c k) -> p c k", k=na)
                bim3 = bim[:, 0:128 * na].rearrange("p (c k) -> p c k", k=na)
                for c0 in range(0, 128, nch):
                    ps = psr.next()
                    for ci in range(nch):
                        c = c0 + ci
                        K.pe.op(lambda e: e.matmul(ps[:, ci * 2 * na:(ci + 1) * 2 * na], xg[0:na, :, c], dft[0:na, 0:2 * na], start=True, stop=True),
                                reads=[xg, dft], writes=[ps])
                    p4 = ps[:, 0:nch * 2 * na].rearrange("p (c r k) -> p c r k", r=2, k=na)
                    ac = p4[:, :, 0, :]
                    as_ = p4[:, :, 1, :]
                    tcb = twc[:, 0:na].unsqueeze(1).broadcast_to([128, nch, na])
                    tsb = tws[:, 0:na].unsqueeze(1).broadcast_to([128, nch, na])
                    t = [tt[q].next() for q in range(4)]
                    tv = [x[:, 0:nch * na].rearrange("p (c k) -> p c k", k=na) for x in t]
                    K.dve.op(lambda e: e.tensor_tensor(tv[0], ac, tcb, op=ALU.mult), reads=[ps, twc], writes=[t[0]])
                    K.dve.op(lambda e: e.tensor_tensor(tv[1], as_, tsb, op=ALU.mult), reads=[ps, tws], writes=[t[1]])
                    K.dve.op(lambda e: e.tensor_tensor(tv[2], ac, tsb, op=ALU.mult), reads=[ps, tws], writes=[t[2]])
                    K.dve.op(lambda e: e.tensor_tensor(tv[3], as_, tcb, op=ALU.mult), reads=[ps, twc], writes=[t[3]])
                    K.pool.op(lambda e: e.tensor_tensor(bre3[:, c0:c0 + nch, :], tv[0], tv[1], op=ALU.subtract), reads=[t[0], t[1]], writes=[bre])
                    K.pool.op(lambda e: e.tensor_tensor(bim3[:, c0:c0 + nch, :], tv[2], tv[3], op=ALU.add), reads=[t[2], t[3]], writes=[bim])
                ncol = 128 * na
                CH = min(2048, 64 * na, ncol)
                for s0 in range(0, ncol, CH):
                    ps_t = pst.next(); qs_t = qst.next()
                    for q0 in range(0, CH, 512):
                        n = min(512, CH - q0)
                        a0 = s0 + q0
                        pp = psP.next(); pq = psQ.next()
                        K.pe.op(lambda e: e.matmul(pp[:, 0:n], c128[:, :], bre[:, a0:a0 + n], start=True, stop=False), reads=[c128, bre], writes=[pp])
                        K.pe.op(lambda e: e.matmul(pp[:, 0:n], ns128[:, :], bim[:, a0:a0 + n], start=False, stop=True), reads=[ns128, bim], writes=[pp])
                        K.pe.op(lambda e: e.matmul(pq[:, 0:n], c128[:, :], bim[:, a0:a0 + n], start=True, stop=False), reads=[c128, bim], writes=[pq])
                        K.pe.op(lambda e: e.matmul(pq[:, 0:n], s128[:, :], bre[:, a0:a0 + n], start=False, stop=True), reads=[s128, bre], writes=[pq])
                        evac(K, ps_t[:, q0:q0 + n], ps_t, pp, pp[:, 0:n])
                        evac(K, qs_t[:, q0:q0 + n], qs_t, pq, pq[:, 0:n])
                    cc0 = s0 // na
                    ncc = CH // na
                    with K.nc.allow_non_contiguous_dma("fnet relayout"):
                        K.pool.dma(dr.pT[g * 128 + cc0:g * 128 + cc0 + ncc, T0:T0 + S].rearrange("c (k2 k1) -> k2 c k1", k1=na),
                                   ps_t[:, 0:CH].rearrange("p (c k) -> p c k", k=na), ps_t, False)
                        K.pool.dma(dr.qqT[g * 128 + cc0:g * 128 + cc0 + ncc, T0:T0 + S].rearrange("c (k2 k1) -> k2 c k1", k1=na),
                                   qs_t[:, 0:CH].rearrange("p (c k) -> p c k", k=na), qs_t, False)
    K.barrier()
    with ExitStack() as es:
        chc = K.sb(es, "f_chc", [128, 128], BF16)
        chs = K.sb(es, "f_chs", [128, 128], BF16)
        pin = Rot([K.sb(es, "f_pin%d" % i, [128, TT], BF16) for i in range(3)])
        qin = Rot([K.sb(es, "f_qin%d" % i, [128, TT], BF16) for i in range(3)])
        fo = Rot([K.sb(es, "f_fo%d" % i, [128, TT], BF16) for i in range(3)])
        psr = Rot(K.psum[0:8])
        for (T0, S) in K.seqs:
            K.sp.dma(chc[:], dr.c["chC_%d" % S], chc, True)
            K.sp.dma(chs[:], dr.c["chnS_%d" % S], chs, True)
            for j in range(S // TT):
                ta = T0 + j * TT
                for g in range(4):
                    p = pin.next(); q = qin.next(); o = fo.next(); ps = psr.next()
                    K.sp.dma(p[:], dr.pT[g * 128:(g + 1) * 128, ta:ta + TT], p, True)
                    K.sp.dma(q[:], dr.qqT[g * 128:(g + 1) * 128, ta:ta + TT], q, True)
                    K.pe.op(lambda e: e.matmul(ps[:, :], chc[:, :], p[:, :], start=True, stop=False), reads=[chc, p], writes=[ps])
                    K.pe.op(lambda e: e.matmul(ps[:, :], chs[:, :], q[:, :], start=False, stop=True), reads=[chs, q], writes=[ps])
                    evac(K, o[:, :], o, ps, ps[:, :])
                    K.pool.dma(dr.mixT[g * 128:(g + 1) * 128, ta:ta + TT], o[:, :], o, False)


def phase_ssd(K, L):
    dr = K.dr
    with ExitStack() as es:
        cw = K.sb(es, "s_cw", [128, 10, 5], F32)
        cb = K.sb(es, "s_cb", [128, 10], F32)
        dtb = K.sb(es, "s_dtb", [128, 24], F32)
        arep = K.sb(es, "s_arep", [128, 24], F32)
        drep = K.sb(es, "s_drep", [128, 12], F32)
        gss = K.sb(es, "s_gss", [128, 6], F32)
        msk = {}
        for nm in ("Uf", "Ub", "Tf", "Tb", "Mf", "Mb", "ones_f"):
            msk[nm] = K.sb(es, "s_" + nm, [128, 128], F32)
            K.sp.dma(msk[nm][:], dr.c[nm], msk[nm], True)
        idb = K.sb(es, "s_idb", [128, 128], BF16)
        K.sp.dma(idb[:], dr.c["ident_b"], idb, True)
        with K.nc.allow_non_contiguous_dma("small per-channel vectors"):
            for jj in range(5):
                K.sp.dma(cw[:, :, jj], dr.conv_w[L, jj].rearrange("(t p) -> p t", p=128), cw, True)
            K.sp.dma(cb[:], dr.conv_b[L].rearrange("(t p) -> p t", p=128), cb, True)
            K.sp.dma(gss[:], dr.ssd_norm_g[L].rearrange("(t p) -> p t", p=128), gss, True)
            K.sp.dma(dtb[:], dr.dt_bias[L:L + 1, :].partition_broadcast(128), dtb, True)
            K.sp.dma(arep[:], dr.a_log[L:L + 1, :].partition_broadcast(128), arep, True)
            K.sp.dma(drep[:], dr.d_skip[L:L + 1, :].partition_broadcast(128), drep, True)
        K.act.op(lambda e: e.activation(arep[:], arep[:], AF.Exp), reads=[arep], writes=[arep])
        K.dve.op(lambda e: e.tensor_scalar_mul(arep[:], arep[:], -1.0), reads=[arep], writes=[arep])

        xh = K.sb(es, "s_xh", [128, 10, TT + 4], BF16)
        cv = K.sb(es, "s_cv", [128, 10, TT], BF16)
        accs = Rot([K.sb(es, "s_acc%d" % i, [128, TT], F32) for i in range(2)])
        ptmp = K.sb(es, "s_ptmp", [128, TT], F32)
        dtin4_r = Rot([K.sb(es, "s_dtin4%d" % i, [128, 4, 64], F32) for i in range(2)])
        dtr4_r = Rot([K.sb(es, "s_dtr4%d" % i, [128, 4, 24], F32) for i in range(2)])
        sa4_r = Rot([K.sb(es, "s_sa4%d" % i, [128, 4, 24], F32) for i in range(2)])
        sb4_r = Rot([K.sb(es, "s_sb4%d" % i, [128, 4, 24], F32) for i in range(2)])
        dt4_r = Rot([K.sb(es, "s_dt4%d" % i, [128, 4, 24], F32) for i in range(2)])
        ad4_r = Rot([K.sb(es, "s_ad4%d" % i, [128, 4, 24], F32) for i in range(2)])
        xs_tok_r = Rot([K.sb(es, "s_xs%d" % i, [128, 768], BF16) for i in range(2)])
        b_tok_r = Rot([K.sb(es, "s_bt%d" % i, [128, 256], BF16) for i in range(2)])
        xd_r = Rot([K.sb(es, "s_xd%d" % i, [128, 768], BF16) for i in range(2)])
        xdd_r = Rot([K.sb(es, "s_xdd%d" % i, [128, 768], BF16) for i in range(2)])
        s1_r = Rot([K.sb(es, "s_s1%d" % i, [128, 24], F32) for i in range(2)])
        s2_r = Rot([K.sb(es, "s_s2%d" % i, [128, 24], F32) for i in range(2)])
        sm_r = Rot([K.sb(es, "s_sm%d" % i, [128, 36], F32) for i in range(2)])
        cbm_r = Rot([K.sb(es, "s_cbm%d" % i, [128, 256], F32) for i in range(2)])
        wt_r = Rot([K.sb(es, "s_wt%d" % i, [128, 768], F32) for i in range(2)])
        ex_r = Rot([K.sb(es, "s_ex%d" % i, [128, 768], F32) for i in range(2)])
        mT_r = Rot([K.sb(es, "s_mT%d" % i, [128, 768], BF16) for i in range(2)])
        st = K.sb(es, "s_st", [128, 768], F32)
        stb = K.sb(es, "s_stb", [128, 768], BF16)
        tmp_r = Rot([K.sb(es, "s_tmp%d" % i, [128, 384], F32) for i in range(2)])
        tmp2_r = Rot([K.sb(es, "s_tmp2%d" % i, [128, 384], F32) for i in range(2)])
        ych = Rot([K.sb(es, "s_y%d" % i, [128, 768], F32) for i in range(2)])
        yfc_r = Rot([K.sb(es, "s_yfc%d" % i, [128, 768], F32) for i in range(2)])
        zc_r = Rot([K.sb(es, "s_zc%d" % i, [128, 768], BF16) for i in range(2)])
        szc_r = Rot([K.sb(es, "s_szc%d" % i, [128, 768], F32) for i in range(2)])
        sqj_r = Rot([K.sb(es, "s_sqj%d" % i, [128, 384], F32) for i in range(2)])
        ss_r = Rot([K.sb(es, "s_ss%d" % i, [128, 2], F32) for i in range(2)])
        ynb_r = Rot([K.sb(es, "s_ynb%d" % i, [128, 768], BF16) for i in range(2)])
        omix = K.sb(es, "s_omix", [128, 6, TT], BF16)
        ps_tr, ps_m, ps_cb, ps_a, ps_b, ps_y, ps_o, ps_s = K.psum

        def do_tile(T0, S, j, d):
            ta = T0 + j * TT
            last = (j == S // TT - 1)
            K.dve.op(lambda e: e.memset(xh[:, :, 0:2], 0.0), writes=[xh])
            K.dve.op(lambda e: e.memset(xh[:, :, TT + 2:TT + 4], 0.0), writes=[xh])
            K.sp.dma(xh[:, :, 2:TT + 2], dr.xbcT[:, ta:ta + TT].rearrange("(t p) s -> p t s", p=128), xh, True)
            with K.nc.allow_non_contiguous_dma("conv halo"):
                if j > 0:
                    K.sp.dma(xh[:, :, 0:2], dr.xbcT[:, ta - 2:ta].rearrange("(t p) s -> p t s", p=128), xh, True)
                if not last:
                    K.sp.dma(xh[:, :, TT + 2:TT + 4], dr.xbcT[:, ta + TT:ta + TT + 2].rearrange("(t p) s -> p t s", p=128), xh, True)
            SSD_EXP = os.environ.get('SSD_EXP', '')
            for t in range(10):
                if 'A' in SSD_EXP:
                    K.act.op(lambda e: e.activation(cv[:, t, :], xh[:, t, 2:TT + 2], AF.Silu, bias=cb[:, t:t + 1], scale=1.0), reads=[xh, cb], writes=[cv])
                    continue
                acc = accs.next()
                eng = K.dve
                eng.op(lambda e: e.tensor_scalar_mul(acc[:, :], xh[:, t, 0:TT], cw[:, t, 0:1]), reads=[xh, cw], writes=[acc])
                for jj in range(1, 5):
                    if eng is K.dve:
                        eng.op(lambda e: e.scalar_tensor_tensor(acc[:, :], xh[:, t, jj:jj + TT], cw[:, t, jj:jj + 1], acc[:, :], op0=ALU.mult, op1=ALU.add),
                               reads=[xh, cw, acc], writes=[acc])
                    else:
                        eng.op(lambda e: e.tensor_scalar_mul(ptmp[:, :], xh[:, t, jj:jj + TT], cw[:, t, jj:jj + 1]), reads=[xh, cw], writes=[ptmp])
                        eng.op(lambda e: e.tensor_tensor(acc[:, :], acc[:, :], ptmp[:, :], op=ALU.add), reads=[acc, ptmp], writes=[acc])
                K.act.op(lambda e: e.activation(cv[:, t, :], acc[:, :], AF.Silu, bias=cb[:, t:t + 1], scale=1.0), reads=[acc, cb], writes=[cv])
            dtin4 = dtin4_r.next(); dtr4 = dtr4_r.next(); sa4 = sa4_r.next(); sb4 = sb4_r.next(); dt4 = dt4_r.next(); ad4 = ad4_r.next()
            K.sp.dma(dtin4[:], dr.dt_tok[ta:ta + TT, :].rearrange("(c p) f -> p c f", p=128), dtin4, True)
            K.dve.op(lambda e: e.tensor_tensor(dtr4[:], dtin4[:, :, 0:24], dtb[:, :].unsqueeze(1).broadcast_to([128, 4, 24]), op=ALU.add), reads=[dtin4, dtb], writes=[dtr4])
            K.dve.op(lambda e: e.tensor_scalar_mul(sa4[:], dtr4[:], -1.0), reads=[dtr4], writes=[sa4])
            K.dve.op(lambda e: e.tensor_tensor(sa4[:], sa4[:], dtr4[:], op=ALU.min), reads=[sa4, dtr4], writes=[sa4])
            K.act.op(lambda e: e.activation(sa4[:], sa4[:], AF.Exp), reads=[sa4], writes=[sa4])
            K.act.op(lambda e: e.activation(sa4[:], sa4[:], AF.Ln, bias=1.0, scale=1.0), reads=[sa4], writes=[sa4])
            K.dve.op(lambda e: e.tensor_scalar_max(sb4[:], dtr4[:], 0.0), reads=[dtr4], writes=[sb4])
            K.dve.op(lambda e: e.tensor_tensor(dt4[:], sa4[:], sb4[:], op=ALU.add), reads=[sa4, sb4], writes=[dt4])
            K.dve.op(lambda e: e.tensor_tensor(ad4[:], dt4[:], arep[:, :].unsqueeze(1).broadcast_to([128, 4, 24]), op=ALU.mult), reads=[dt4, arep], writes=[ad4])
            order = range(4) if d == 0 else range(3, -1, -1)
            U = msk["Uf"] if d == 0 else msk["Ub"]
            Tm = msk["Tf"] if d == 0 else msk["Tb"]
            Mm = msk["Mf"] if d == 0 else msk["Mb"]
            for c4 in order:
                cs = slice(c4 * 128, (c4 + 1) * 128)
                xs_tok = xs_tok_r.next()
                b_tok = b_tok_r.next()
                xd = xd_r.next()
                xdd = xdd_r.next()
                s1 = s1_r.next()
                s2 = s2_r.next()
                sm = sm_r.next()
                cbm = cbm_r.next()
                yfc = yfc_r.next()
                zc = zc_r.next()
                szc = szc_r.next()
                sqj = sqj_r.next()
                ss = ss_r.next()
                ynb = ynb_r.next()
                tc0 = ta + c4 * 128
                trb = ps_tr[:, :].bitcast(BF16)
                for t in range(8):
                    K.pe.op(lambda e: e.transpose(trb[:, t * 128:(t + 1) * 128], cv[:, t, cs], idb[:]), reads=[cv, idb], writes=[ps_tr])
                K.act.op(lambda e: e.activation(xs_tok[:, :], trb[:, 0:768], AF.Copy), reads=[ps_tr], writes=[xs_tok])
                K.act.op(lambda e: e.activation(b_tok[:, :], trb[:, 768:1024], AF.Copy), reads=[ps_tr], writes=[b_tok])
                dcol = slice(d * 12, (d + 1) * 12)
                K.dve.op(lambda e: e.tensor_tensor(xd[:, :].rearrange("p (h q) -> p h q", q=64), trb[:, 0:768].rearrange("p (h q) -> p h q", q=64),
                                                   dt4[:, c4, dcol].unsqueeze(2).broadcast_to([128, 12, 64]), op=ALU.mult), reads=[ps_tr, dt4], writes=[xd])
                K.pe.op(lambda e: e.matmul(ps_m[:, 0:12], U[:, :], ad4[:, c4, dcol], start=True, stop=True), reads=[U, ad4], writes=[ps_m])
                K.pe.op(lambda e: e.matmul(ps_m[:, 12:24], Tm[:, :], ad4[:, c4, dcol], start=True, stop=True), reads=[Tm, ad4], writes=[ps_m])
                K.pe.op(lambda e: e.matmul(ps_m[:, 24:36], msk["ones_f"][:, :], ad4[:, c4, dcol], start=True, stop=True), reads=[msk["ones_f"], ad4], writes=[ps_m])
                K.act.op(lambda e: e.activation(sm[:, :], ps_m[:, 0:36], AF.Exp), reads=[ps_m], writes=[sm])
                K.dve.op(lambda e: e.tensor_tensor(xdd[:, :].rearrange("p (h q) -> p h q", q=64), xd[:, :].rearrange("p (h q) -> p h q", q=64),
                                                   sm[:, 0:12].unsqueeze(2).broadcast_to([128, 12, 64]), op=ALU.mult), reads=[xd, sm], writes=[xdd])
                for g in range(2):
                    K.pe.op(lambda e: e.matmul(ps_cb[:, g * 128:(g + 1) * 128], cv[:, 6 + g, cs], cv[:, 8 + g, cs], start=True, stop=True),
                            reads=[cv], writes=[ps_cb])
                K.dve.op(lambda e: e.tensor_tensor(cbm[:, :].rearrange("p (g l) -> p g l", g=2), ps_cb[:, 0:256].rearrange("p (g l) -> p g l", g=2),
                                                   Mm[:, :].unsqueeze(1).broadcast_to([128, 2, 128]), op=ALU.mult), reads=[ps_cb, Mm], writes=[cbm])
                y = ych.next()
                for g in range(0 if 'B' in SSD_EXP else 2):
                    hc = slice(d * 12 + g * 6, d * 12 + g * 6 + 6)
                    wt = wt_r.next()
                    ex = ex_r.next()
                    mT = mT_r.next()
                    tmp = tmp_r.next()
                    tmp2 = tmp2_r.next()
                    gc = slice(g * 384, (g + 1) * 384)
                    K.pool.op(lambda e: e.tensor_tensor(wt[:, :].rearrange("p (r l) -> p r l", r=6), ad4[:, c4, hc].unsqueeze(2).broadcast_to([128, 6, 128]),
                                                        Tm[:, :].unsqueeze(1).broadcast_to([128, 6, 128]), op=ALU.mult), reads=[ad4, Tm], writes=[wt])
                    K.pe.op(lambda e: e.matmul(ps_a[:, 0:384], U[:, :], wt[:, 0:384], start=True, stop=True), reads=[U, wt], writes=[ps_a])
                    K.pe.op(lambda e: e.matmul(ps_b[:, 0:384], U[:, :], wt[:, 384:768], start=True, stop=True), reads=[U, wt], writes=[ps_b])
                    K.act.op(lambda e: e.activation(ex[:, 0:384], ps_a[:, 0:384], AF.Exp), reads=[ps_a], writes=[ex])
                    K.act.op(lambda e: e.activation(ex[:, 384:768], ps_b[:, 0:384], AF.Exp), reads=[ps_b], writes=[ex])
                    K.dve.op(lambda e: e.tensor_tensor(mT[:, :].rearrange("p (r l) -> p r l", r=6), ex[:, :].rearrange("p (r l) -> p r l", r=6),
                                                       cbm[:, g * 128:(g + 1) * 128].unsqueeze(1).broadcast_to([128, 6, 128]), op=ALU.mult),
                             reads=[ex, cbm], writes=[mT])
                    for r in range(6):
                        K.pe.op(lambda e: e.matmul(ps_y[:, r * 64:(r + 1) * 64], mT[:, r * 128:(r + 1) * 128],
                                                   xd[:, (g * 6 + r) * 64:(g * 6 + r + 1) * 64], start=True, stop=True), reads=[mT, xd], writes=[ps_y])
                    K.pe.op(lambda e: e.matmul(ps_o[:, 0:384], cv[:, 8 + g, cs], stb[:, gc], start=True, stop=True), reads=[cv, stb], writes=[ps_o])
                    K.dve.op(lambda e: e.tensor_tensor(tmp[:, :].rearrange("p (r q) -> p r q", q=64), ps_o[:, 0:384].rearrange("p (r q) -> p r q", q=64),
                                                       sm[:, 12 + g * 6:12 + g * 6 + 6].unsqueeze(2).broadcast_to([128, 6, 64]), op=ALU.mult),
                             reads=[ps_o, sm], writes=[tmp])
                    K.dve.op(lambda e: e.tensor_tensor(y[:, gc], tmp[:, :], ps_y[:, 0:384], op=ALU.add), reads=[tmp, ps_y], writes=[y])
                    K.pe.op(lambda e: e.matmul(ps_s[:, 0:384], b_tok[:, g * 128:(g + 1) * 128], xdd[:, gc], start=True, stop=True),
                            reads=[b_tok, xdd], writes=[ps_s])
                    K.pool.op(lambda e: e.tensor_tensor(tmp2[:, :].rearrange("p (r q) -> p r q", q=64), st[:, gc].rearrange("p (r q) -> p r q", q=64),
                                                        sm[:, 24 + g * 6:24 + g * 6 + 6].unsqueeze(2).broadcast_to([128, 6, 64]), op=ALU.mult),
                              reads=[st, sm], writes=[tmp2])
                    K.dve.op(lambda e: e.tensor_tensor(st[:, gc], tmp2[:, :], ps_s[:, 0:384], op=ALU.add), reads=[tmp2, ps_s], writes=[st])
                    K.act.op(lambda e: e.activation(stb[:, gc], st[:, gc], AF.Copy), reads=[st], writes=[stb])
                if d == 0:
                    K.pool.dma(dr.yf[tc0:tc0 + 128, :], y[:, :], y, False)
                else:
                    K.sp.dma(yfc[:], dr.yf[tc0:tc0 + 128, :], yfc, True)
                    K.sp.dma(zc[:], dr.z_tok[tc0:tc0 + 128, :], zc, True)
                    K.pool.op(lambda e: e.tensor_tensor(y[:, :], y[:, :], yfc[:, :], op=ALU.add), reads=[y, yfc], writes=[y])
                    K.dve.op(lambda e: e.tensor_tensor(szc[:, :].rearrange("p (h q) -> p h q", q=64), xs_tok[:, :].rearrange("p (h q) -> p h q", q=64),
                                                       drep[:, :].unsqueeze(2).broadcast_to([128, 12, 64]), op=ALU.mult), reads=[xs_tok, drep], writes=[szc])
                    K.pool.op(lambda e: e.tensor_tensor(y[:, :], y[:, :], szc[:, :], op=ALU.add), reads=[y, szc], writes=[y])
                    K.act.op(lambda e: e.activation(szc[:, :], zc[:, :], AF.Silu), reads=[zc], writes=[szc])
                    K.dve.op(lambda e: e.tensor_tensor(y[:, :], y[:, :], szc[:, :], op=ALU.mult), reads=[y, szc], writes=[y])
                    for g in range(2):
                        K.act.op(lambda e: e.activation(sqj[:, :], y[:, g * 384:(g + 1) * 384], AF.Square, accum_out=ss[:, g:g + 1]),
                                 reads=[y], writes=[sqj, ss])
                    K.act.op(lambda e: e.activation(ss[:, :], ss[:, :], AF.Sqrt, bias=EPS, scale=1.0 / 384.0), reads=[ss], writes=[ss])
                    K.dve.op(lambda e: e.reciprocal(ss[:, :], ss[:, :]), reads=[ss], writes=[ss])
                    for g in range(2):
                        K.dve.op(lambda e: e.tensor_scalar_mul(ynb[:, g * 384:(g + 1) * 384], y[:, g * 384:(g + 1) * 384], ss[:, g:g + 1]),
                                 reads=[y, ss], writes=[ynb])
                    for t in range(6):
                        K.pe.op(lambda e: e.transpose(trb[:, t * 128:(t + 1) * 128], ynb[:, t * 128:(t + 1) * 128], idb[:]), reads=[ynb, idb], writes=[ps_tr])
                    for t in range(6):
                        K.dve.op(lambda e: e.tensor_scalar_mul(omix[:, t, cs], trb[:, t * 128:(t + 1) * 128], gss[:, t:t + 1]),
                                 reads=[ps_tr, gss], writes=[omix])
            if d == 1:
                K.pool.dma(dr.mixT[1280:2048, ta:ta + TT].rearrange("(t p) s -> p t s", p=128), omix[:], omix, False)

        for (T0, S) in K.seqs:
            for d in (0, 1):
                K.dve.op(lambda e: e.memset(st[:], 0.0), writes=[st])
                K.dve.op(lambda e: e.memset(stb[:], 0.0), writes=[stb])
                if d == 0:
                    K.barrier() if False else None
                tiles = range(S // TT) if d == 0 else range(S // TT - 1, -1, -1)
                for j in tiles:
                    do_tile(T0, S, j, d)
                if d == 0:
                    K.barrier()


def phase_attn(K, L):
    dr = K.dr
    smax = max(S for _, S in K.seqs)
    with ExitStack() as es:
        kT = K.sb(es, "t_kT", [128, smax], BF16)
        kpe = K.sb(es, "t_kpe", [64, smax], BF16)
        vv = K.sb(es, "t_v", [128, smax // 128, 128], BF16)
        qn = Rot([K.sb(es, "t_qn%d" % i, [128, TT], BF16) for i in range(2)])
        qp = Rot([K.sb(es, "t_qp%d" % i, [64, TT], BF16) for i in range(2)])
        pT = Rot([K.sb(es, "t_p%d" % i, [128, 2 * TT], BF16) for i in range(4)])
        onesf = K.sb(es, "t_onesf", [128, 128], F32)
        daccs = Rot([K.sb(es, "t_dacc%d" % i, [128, 2 * TT], F32) for i in range(2)])
        rden = Rot([K.sb(es, "t_rd%d" % i, [128, TT], F32) for i in range(2)])
        ob = Rot([K.sb(es, "t_ob%d" % i, [128, TT], BF16) for i in range(2)])
        K.sp.dma(onesf[:], dr.c["ones_f"], onesf, True)
        psS = Rot([K.psum2[0], K.psum2[1], K.psum2[3]])
        psOD = Rot([(K.psum[4], K.psum[5])])
        for (T0, S) in K.seqs:
            nj = S // 128
            K.sp.dma(kpe[:, 0:S], dr.kpeT[:, T0:T0 + S], kpe, True)
            for h in range(6):
                K.sp.dma(kT[:, 0:S], dr.kT[h, :, T0:T0 + S], kT, True)
                K.sp.dma(vv[:, 0:nj, :], dr.v_tok[T0:T0 + S, h * 128:(h + 1) * 128].rearrange("(j p) d -> p j d", p=128), vv, True)
                for qc in range(S // TT):
                    ta = T0 + qc * TT
                    q1 = qn.next(); q2 = qp.next()
                    K.sp.dma(q1[:], dr.qT[h, 0:128, ta:ta + TT], q1, True)
                    K.sp.dma(q2[:], dr.qT[h, 128:192, ta:ta + TT], q2, True)
                    psO, psD = psOD.next()
                    dacc = daccs.next()
                    sbanks = {}

                    def qk(jp):
                        ps = psS.next()
                        for hh in range(2):
                            j = 2 * jp + hh
                            K.pe.op(lambda e: e.matmul(ps[:, hh * 512:(hh + 1) * 512], kT[:, j * 128:(j + 1) * 128], q1[:, :], start=True, stop=False),
                                    reads=[kT, q1], writes=[ps])
                            K.pe.op(lambda e: e.matmul(ps[:, hh * 512:(hh + 1) * 512], kpe[0:64, j * 128:(j + 1) * 128], q2[0:64, :], start=False, stop=True),
                                    reads=[kpe, q2], writes=[ps])
                        sbanks[jp] = ps

                    def pv(jp):
                        ps = sbanks.pop(jp)
                        p = pT.next()
                        K.act.op(lambda e: e.activation(p[:, :], ps[:, :], AF.Exp), reads=[ps], writes=[p])
                        if jp == 0:
                            K.dve.op(lambda e: e.tensor_copy(dacc[:, :], p[:, :]), reads=[p], writes=[dacc])
                        else:
                            K.dve.op(lambda e: e.tensor_tensor(dacc[:, :], dacc[:, :], p[:, :], op=ALU.add), reads=[dacc, p], writes=[dacc])
                        for hh in range(2):
                            j = 2 * jp + hh
                            K.pe.op(lambda e: e.matmul(psO[:, :], vv[:, j, :], p[:, hh * 512:(hh + 1) * 512], start=(j == 0), stop=(j == nj - 1)),
                                    reads=[vv, p], writes=[psO])

                    LOOK = int(os.environ.get('ATT_LOOK', '2'))
                    npair = nj // 2
                    for jp in range(min(LOOK, npair)):
                        qk(jp)
                    for jp in range(npair):
                        if jp + LOOK < npair:
                            qk(jp + LOOK)
                        pv(jp)
                    K.pe.op(lambda e: e.matmul(psD[:, :], onesf[:, :], dacc[:, 0:512], start=True, stop=False), reads=[onesf, dacc], writes=[psD])
                    K.pe.op(lambda e: e.matmul(psD[:, :], onesf[:, :], dacc[:, 512:1024], start=False, stop=True), reads=[onesf, dacc], writes=[psD])
                    rd = rden.next(); o = ob.next()
                    K.dve.op(lambda e: e.reciprocal(rd[:, :], psD[:, :]), reads=[psD], writes=[rd])
                    K.dve.op(lambda e: e.tensor_tensor(o[:, :], psO[:, :], rd[:, :], op=ALU.mult), reads=[psO, rd], writes=[o])
                    K.pool.dma(dr.mixT[512 + h * 128:512 + (h + 1) * 128, ta:ta + TT], o[:, :], o, False)


def phase_w(K, L):
    dr = K.dr
    with ExitStack() as es:
        x1 = K.sb(es, "w_x1", [128, NKT, TT], F32)
        acc = K.sb(es, "w_acc", [128, NKT, TT], F32)
        hb = K.sb(es, "w_hb", [128, NKT, TT], BF16)
        aT = K.sb(es, "w_aT", [128, 64, TT], BF16)
        sqs = Rot([K.sb(es, "w_sq%d" % i, [128, TT], BF16) for i in range(3)])
        wbs = Rot([K.sb(es, "w_wb%d" % i, [128, NKT, 512], BF16) for i in range(2)])
        rstd = K.sb(es, "w_rstd", [128, TT], F32)
        onesD = K.sb(es, "w_onesD", [128, 128], BF16)
        g1 = K.sb(es, "w_g1", [128, NKT], F32)
        g2 = K.sb(es, "w_g2", [128, NKT], F32)
        g3 = K.sb(es, "w_g3", [128, NKT], F32)
        tmp = Rot([K.sb(es, "w_tmp%d" % i, [128, TT], F32) for i in range(4)])
        K.sp.dma(onesD[:], dr.c["onesD"], onesD, True)
        with K.nc.allow_non_contiguous_dma("small gain vectors"):
            K.sp.dma(g1[:], dr.post_mix_g[L].rearrange("(kt p) -> p kt", p=128), g1, True)
            K.sp.dma(g2[:], dr.pre_ffn_g[L].rearrange("(kt p) -> p kt", p=128), g2, True)
            K.sp.dma(g3[:], dr.post_ffn_g[L].rearrange("(kt p) -> p kt", p=128), g3, True)
        psr = Rot(K.psum[1:8])
        ps_st = K.psum[0]

        def post_norm_add(gv, dst_is_x1):
            rms_rep(K, ps_st, lambda kt: acc[:, kt, :], NKT, onesD, sqs, [acc], rstd)
            for kt in range(NKT):
                t = tmp.next()
                K.dve.op(lambda e: e.scalar_tensor_tensor(t[:, :], acc[:, kt, :], gv[:, kt:kt + 1], rstd[:], op0=ALU.mult, op1=ALU.mult),
                         reads=[acc, gv, rstd], writes=[t])
                eng = K.pool if kt % 4 == 3 else K.dve
                eng.op(lambda e: e.tensor_tensor(x1[:, kt, :], x1[:, kt, :], t[:, :], op=ALU.add), reads=[x1, t], writes=[x1])

        for ti in range(K.T // TT):
            ta = ti * TT
            K.sp.dma(x1[:], dr.xT[:, ta:ta + TT].rearrange("(kt p) t -> p kt t", p=128), x1, True)
            K.sp.dma(hb[:], dr.mixT[:, ta:ta + TT].rearrange("(kt p) t -> p kt t", p=128), hb, True)
            for cg in range(4):
                wb = wbs.next()
                K.sp.dma(wb[:], dr.w_out_b[L, cg], wb, True)
                for m in range(4):
                    ps = psr.next()
                    for kt in range(NKT):
                        K.pe.op(lambda e: e.matmul(ps[:, :], wb[:, kt, m * 128:(m + 1) * 128], hb[:, kt, :], start=(kt == 0), stop=(kt == NKT - 1)),
                                reads=[wb, hb], writes=[ps])
                    evac(K, acc[:, cg * 4 + m, :], acc, ps, ps[:, :])
            post_norm_add(g1, True)
            rms_rep(K, ps_st, lambda kt: x1[:, kt, :], NKT, onesD, sqs, [x1], rstd)
            for kt in range(NKT):
                K.dve.op(lambda e: e.scalar_tensor_tensor(hb[:, kt, :], x1[:, kt, :], g2[:, kt:kt + 1], rstd[:], op0=ALU.mult, op1=ALU.mult),
                         reads=[x1, g2, rstd], writes=[hb])
            for cg in range(16):
                wb = wbs.next()
                K.sp.dma(wb[:], dr.w_ff1_b[L, cg], wb, True)
                for m in range(4):
                    ps = psr.next()
                    for kt in range(NKT):
                        K.pe.op(lambda e: e.matmul(ps[:, :], wb[:, kt, m * 128:(m + 1) * 128], hb[:, kt, :], start=(kt == 0), stop=(kt == NKT - 1)),
                                reads=[wb, hb], writes=[ps])
                    t = tmp.next()
                    K.act.op(lambda e: e.activation(t[:, :], ps[:, :], AF.Relu), reads=[ps], writes=[t])
                    K.dve.op(lambda e: e.tensor_tensor(aT[:, cg * 4 + m, :], t[:, :], t[:, :], op=ALU.mult), reads=[t], writes=[aT])
            for cg in range(4):
                pss = [psr.next() for _ in range(4)]
                for kb in range(4):
                    wb = wbs.next()
                    K.sp.dma(wb[:], dr.w_ff2_b[L, cg * 4 + kb], wb, True)
                    for m in range(4):
                        for kt in range(NKT):
                            K.pe.op(lambda e: e.matmul(pss[m][:, :], wb[:, kt, m * 128:(m + 1) * 128], aT[:, kb * 16 + kt, :],
                                                       start=(kb == 0 and kt == 0), stop=(kb == 3 and kt == NKT - 1)),
                                    reads=[wb, aT], writes=[pss[m]])
                for m in range(4):
                    evac(K, acc[:, cg * 4 + m, :], acc, pss[m], pss[m][:, :])
            post_norm_add(g3, True)
            K.pool.dma(dr.xT[:, ta:ta + TT].rearrange("(kt p) t -> p kt t", p=128), x1[:], x1, False)


N_CORES = 8
SEQ_LENS = (16384, 2048, 2048)
DEPTH = 2
_CACHE = {}


def kernel(x_prompt, x_sample, pre_mix_g, w_in, q_norm_g, w_q_up, kv_norm_g, w_kv_up,
           conv_w, conv_b, dt_bias_f, dt_bias_b, a_log_f, a_log_b, d_skip, ssd_norm_g,
           w_out, post_mix_g, pre_ffn_g, w_ff1, w_ff2, post_ffn_g):
    f = lambda a: np.ascontiguousarray(np.asarray(a, dtype=np.float32))
    x_prompt = f(x_prompt)
    x_sample = f(x_sample)
    if "nc" not in _CACHE:
        _CACHE["nc"] = build_program(list(SEQ_LENS), DEPTH)
    nc, consts = _CACHE["nc"]
    shared = {
        "pre_mix_g": f(pre_mix_g), "w_in": f(w_in), "q_norm_g": f(q_norm_g), "w_q_up": f(w_q_up),
        "kv_norm_g": f(kv_norm_g), "w_kv_up": f(w_kv_up), "conv_w": f(conv_w), "conv_b": f(conv_b),
        "dt_bias": np.ascontiguousarray(np.concatenate([f(dt_bias_f), f(dt_bias_b)], axis=1)),
        "a_log": np.ascontiguousarray(np.concatenate([f(a_log_f), f(a_log_b)], axis=1)),
        "d_skip": f(d_skip), "ssd_norm_g": f(ssd_norm_g), "w_out": f(w_out), "post_mix_g": f(post_mix_g),
        "pre_ffn_g": f(pre_ffn_g), "w_ff1": f(w_ff1), "w_ff2": f(w_ff2), "post_ffn_g": f(post_ffn_g),
    }
    for k, v in consts.items():
        shared["c_" + k] = v
    in_maps = []
    for c in range(N_CORES):
        xp = x_prompt[c] if c < x_prompt.shape[0] else np.zeros_like(x_prompt[0])
        x = np.concatenate([xp, x_sample[2 * c], x_sample[2 * c + 1]], axis=0)
        m = dict(shared)
        m["x"] = np.ascontiguousarray(x)
        in_maps.append(m)
    res = run_bass_kernel_spmd(nc, in_maps, core_ids=list(range(N_CORES)))
    ys = [np.asarray(r["y"]) for r in res.results]
    SP = SEQ_LENS[0]
    SS = SEQ_LENS[1]
    y_prompt = np.stack([ys[c][0:SP] for c in range(x_prompt.shape[0])], axis=0).astype(np.float32)
    y_sample = np.empty_like(x_sample)
    for c in range(N_CORES):
        y_sample[2 * c] = ys[c][SP:SP + SS]
        y_sample[2 * c + 1] = ys[c][SP + SS:SP + 2 * SS]
    return (y_prompt, y_sample)
```

```python
from contextlib import ExitStack
import os
import numpy as np
import ml_dtypes
import concourse.bass as bass
import concourse.mybir as mybir
from concourse.bass_utils import run_bass_kernel_spmd

F32 = mybir.dt.float32
BF16 = mybir.dt.bfloat16
AF = mybir.ActivationFunctionType
ALU = mybir.AluOpType
AX = mybir.AxisListType
import os as _os
SAME_ENG_NOWAIT = _os.environ.get('SAME_ENG_NOWAIT', '0') == '1'


class Buf:
    def __init__(self, K, name, t):
        self.K = K
        self.name = name
        self.t = t
        self.w = None
        self.rs = {}
        self.dsem = None
        self.is_psum = False

    def __getitem__(self, k):
        return self.t[k]


class Eng:
    def __init__(self, K, name, eng, is_pe=False):
        self.K = K
        self.name = name
        self.eng = eng
        self.sem = K.new_sem("e_" + name)
        self.cnt = 0
        self.seen = {}
        self.is_pe = is_pe

    def wait(self, sem, val):
        if val <= 0:
            return
        key = id(sem)
        if self.seen.get(key, 0) < val:
            self.eng.wait_ge(sem, val)
            self.seen[key] = val

    def op(self, fn, reads=(), writes=(), inc=True):
        for b in reads:
            if b.w is not None:
                if not ((self.is_pe or SAME_ENG_NOWAIT) and b.w[0] is self.sem):
                    self.wait(*b.w)
            if b.is_psum:
                for s, v in b.rs.values():
                    if s is not self.sem:
                        self.wait(s, v)
        for b in writes:
            if b.w is not None:
                if not ((self.is_pe or SAME_ENG_NOWAIT) and b.w[0] is self.sem):
                    self.wait(*b.w)
            for s, v in b.rs.values():
                if s is not self.sem or not self.is_pe:
                    self.wait(s, v)
        ins = fn(self.eng)
        inc = True
        if inc:
            self.cnt += 1
            ins.then_inc(self.sem, 1)
            val = self.cnt
            self.K.latest[id(self.sem)] = (self.sem, val)
        else:
            val = self.cnt + 1
        for b in reads:
            b.rs[id(self.sem)] = (self.sem, val)
        for b in writes:
            b.w = (self.sem, val)
            b.rs = {}
        return ins

    def dma(self, out, in_, buf, load):
        if buf.dsem is None:
            buf.dsem = {}
        if self.name not in buf.dsem:
            buf.dsem[self.name] = self.K.get_dsem(self.name, buf.name)
        ent = buf.dsem[self.name]
        sem = ent[0]
        if buf.w is not None:
            self.wait(*buf.w)
        if load:
            for s, v in buf.rs.values():
                self.wait(s, v)
        ent[1] += 16
        tot = ent[1]
        self.eng.dma_start(out=out, in_=in_).then_inc(sem, 16)
        self.K.latest[id(sem)] = (sem, tot)
        if load:
            buf.w = (sem, tot)
            buf.rs = {}
        else:
            buf.rs[id(sem)] = (sem, tot)


class Kern:
    def __init__(self, nc, es):
        self.nc = nc
        self.es = es
        self.latest = {}
        self.nsem = 0
        self.sem_pool = {}
        self.phase_bufs = []
        self.pe = Eng(self, "pe", nc.tensor, is_pe=True)
        self.act = Eng(self, "act", nc.scalar)
        self.dve = Eng(self, "dve", nc.vector)
        self.pool = Eng(self, "pool", nc.gpsimd)
        self.sp = Eng(self, "sp", nc.sync)
        self.engs = [self.pe, self.act, self.dve, self.pool, self.sp]
        self.psum = []
        self.psum2 = []
        for i in range(4):
            big = self.es.enter_context(self.nc.psum_tensor("psb%d" % i, [128, 1024], F32))
            b2 = Buf(self, "psb%d" % i, big)
            b2.is_psum = True
            self.psum2.append(b2)
            for hh in range(2):
                b = Buf(self, "ps%d" % (2 * i + hh), big[:, hh * 512:(hh + 1) * 512])
                b.is_psum = True
                self.psum.append(b)
        self._flip = 0

    def new_sem(self, name):
        self.nsem += 1
        return self.es.enter_context(self.nc.semaphore(name + "_%d" % self.nsem))

    def get_dsem(self, engname, bufname):
        pool = self.sem_pool.setdefault(engname, [])
        if pool:
            return pool.pop()
        return [self.new_sem("d_" + engname), 0]

    def recycle(self):
        for b in self.phase_bufs:
            if b.dsem:
                for en, ent in b.dsem.items():
                    self.sem_pool.setdefault(en, []).append(ent)
                b.dsem = None
        self.phase_bufs = []

    def new_psum(self, name):
        t = self.es.enter_context(self.nc.psum_tensor(name, [128, 512], F32))
        b = Buf(self, name, t)
        b.is_psum = True
        return b

    def sb(self, es, name, shape, dtype):
        self.nsb = getattr(self, "nsb", 0) + 1
        name = "%s_%d" % (name, self.nsb)
        t = es.enter_context(self.nc.sbuf_tensor(name, list(shape), dtype))
        b = Buf(self, name, t)
        self.phase_bufs.append(b)
        return b

    def barrier(self, recycle=True):
        for e in self.engs:
            for s, v in list(self.latest.values()):
                e.wait(s, v)

    def ev(self):
        self._flip ^= 1
        return self.dve if self._flip else self.act


D = 2048
NKT = 16
TT = 512
IN_DIM = 3672
WIN_COLS = IN_DIM + 64
D_FF = 8192
EPS = 1e-6
QK_HEAD = 192
ROPE_THETA = 10000.0
C_U, C_CQ, C_CKV, C_KPE, C_Z, C_XBC, C_DT = 0, 512, 1024, 1536, 1600, 2368, 3648


def host_constants(seq_lens):
    bf = ml_dtypes.bfloat16
    c = {}
    c["ident_b"] = np.eye(128, dtype=np.float32).astype(bf)
    c["ident_f"] = np.eye(128, dtype=np.float32)
    c["onesD"] = np.full((128, 128), 1.0 / D, np.float32).astype(bf)
    c["ones512"] = np.full((128, 128), 1.0 / 512, np.float32).astype(bf)
    c["ones1"] = np.ones((128, 128), np.float32).astype(bf)
    c["ones_f"] = np.ones((128, 128), np.float32)
    smax = max(seq_lens)
    pos = np.arange(smax, dtype=np.float32)
    inv = (ROPE_THETA ** (-np.arange(0, 64, 2, dtype=np.float32) / 64)).astype(np.float32)
    ang = (pos[None, :] * inv[:, None]).astype(np.float32)
    c["cos128"] = np.tile(np.cos(ang), (4, 1)).astype(np.float32)
    c["sin128"] = np.tile(np.sin(ang), (4, 1)).astype(np.float32)
    t = np.arange(128)
    c["Uf"] = (t[:, None] > t[None, :]).astype(np.float32)
    c["Ub"] = (t[:, None] < t[None, :]).astype(np.float32)
    c["Tf"] = (t[:, None] <= t[None, :]).astype(np.float32)
    c["Tb"] = (t[:, None] >= t[None, :]).astype(np.float32)
    c["Mf"] = (t[None, :] >= t[:, None]).astype(np.float32)
    c["Mb"] = (t[None, :] <= t[:, None]).astype(np.float32)
    a = np.arange(128, dtype=np.float64)
    th = 2 * np.pi * np.outer(a, a) / 128.0
    c["C128"] = np.cos(th).astype(np.float32).astype(bf)
    c["S128"] = np.sin(th).astype(np.float32).astype(bf)
    c["nS128"] = (-np.sin(th)).astype(np.float32).astype(bf)
    for S in sorted(set(seq_lens)):
        na = S // 128
        aa = np.arange(na, dtype=np.float64)
        tha = 2 * np.pi * np.outer(aa, aa) / na
        cs = np.zeros((128, 2 * na), np.float32)
        cs[:na, :na] = np.cos(tha)
        cs[:na, na:] = np.sin(tha)
        c["dftA_%d" % S] = cs.astype(bf)
        b = np.arange(128, dtype=np.float64)
        tw = 2 * np.pi * np.outer(b, aa) / S
        c["twc_%d" % S] = np.cos(tw).astype(np.float32)
        c["tws_%d" % S] = np.sin(tw).astype(np.float32)
        sc = 1.0 / np.sqrt(S * 128.0)
        c["chC_%d" % S] = (np.cos(th) * sc).astype(np.float32).astype(bf)
        c["chnS_%d" % S] = (-np.sin(th) * sc).astype(np.float32).astype(bf)
    return c


class DR:
    pass


def build_program(seq_lens, depth, debug_outs=(), phases='paftswe'):
    nc = bass.Bass("TRN2", target_bir_lowering=False)
    T = sum(seq_lens)
    dr = DR()

    def din(name, shape, dt=F32):
        return nc.dram_tensor(name, list(shape), dt, kind="ExternalInput").ap()

    def dscr(name, shape, dt):
        kind = "ExternalOutput" if name in debug_outs else "Internal"
        return nc.dram_tensor(name, list(shape), dt, kind=kind).ap()

    dr.x = din("x", [T, D])
    dr.y = nc.dram_tensor("y", [T, D], F32, kind="ExternalOutput").ap()
    Ld = depth
    dr.pre_mix_g = din("pre_mix_g", [Ld, D]); dr.w_in = din("w_in", [Ld, D, IN_DIM])
    dr.q_norm_g = din("q_norm_g", [Ld, 512]); dr.w_q_up = din("w_q_up", [Ld, 512, 1152])
    dr.kv_norm_g = din("kv_norm_g", [Ld, 512]); dr.w_kv_up = din("w_kv_up", [Ld, 512, 1536])
    dr.conv_w = din("conv_w", [Ld, 5, 1280]); dr.conv_b = din("conv_b", [Ld, 1280])
    dr.dt_bias = din("dt_bias", [Ld, 24]); dr.a_log = din("a_log", [Ld, 24])
    dr.d_skip = din("d_skip", [Ld, 12]); dr.ssd_norm_g = din("ssd_norm_g", [Ld, 768])
    dr.w_out = din("w_out", [Ld, D, D]); dr.post_mix_g = din("post_mix_g", [Ld, D])
    dr.pre_ffn_g = din("pre_ffn_g", [Ld, D]); dr.w_ff1 = din("w_ff1", [Ld, D, D_FF])
    dr.w_ff2 = din("w_ff2", [Ld, D_FF, D]); dr.post_ffn_g = din("post_ffn_g", [Ld, D])
    consts = host_constants(seq_lens)
    dr.c = {}
    for k, v in consts.items():
        dr.c[k] = din("c_" + k, v.shape, BF16 if v.dtype == ml_dtypes.bfloat16 else F32)
    dr.xT = dscr("xT", [D, T], F32)
    dr.w_in_b = dscr("w_in_b", [Ld, D, WIN_COLS], BF16)
    dr.w_q_b = dscr("w_q_b", [Ld, 512, 1536], BF16)
    dr.w_kv_b = dscr("w_kv_b", [Ld, 512, 1536], BF16)
    dr.w_out_b = dscr("w_out_b", [Ld, 4, 128, NKT, 512], BF16)
    dr.w_ff1_b = dscr("w_ff1_b", [Ld, 16, 128, NKT, 512], BF16)
    dr.w_ff2_b = dscr("w_ff2_b", [Ld, 16, 128, NKT, 512], BF16)
    dr.u_d = dscr("u_d", [4, T, 128], BF16)
    dr.qT = dscr("qT", [6, 192, T], BF16)
    dr.kT = dscr("kT", [6, 128, T], BF16)
    dr.kpeT = dscr("kpeT", [64, T], BF16)
    dr.v_tok = dscr("v_tok", [T, 768], BF16)
    dr.xbcT = dscr("xbcT", [1280, T], BF16)
    dr.z_tok = dscr("z_tok", [T, 768], BF16)
    dr.dt_tok = dscr("dt_tok", [T, 64], F32)
    dr.mixT = dscr("mixT", [D, T], BF16)
    dr.pT = dscr("pT", [512, T], BF16)
    dr.qqT = dscr("qqT", [512, T], BF16)
    dr.yf = dscr("yf", [T, 768], F32)
    dr.cvT = dscr("cvT", [1280, T], BF16)

    with ExitStack() as es:
        K = Kern(nc, es)
        K.dr = dr
        K.seqs = []
        t0 = 0
        for S in seq_lens:
            K.seqs.append((t0, S))
            t0 += S
        K.T = T
        if 'p' in phases:
            prologue(K, depth)
            K.barrier()
            K.recycle()
        for L in range(depth):
            for ch, fn in (('a', phase1), ('f', phase_fnet), ('t', phase_attn), ('s', phase_ssd), ('w', phase_w)):
                if ch in phases:
                    fn(K, L)
                    K.barrier()
                    K.recycle()
        if 'e' in phases:
            epilogue(K)
            K.barrier()
    return nc, consts


def dma_dd(K, out, in_):
    if not hasattr(K, "dd_sem"):
        K.dd_sem = K.new_sem("dd")
        K.dd_tot = 0
    K.dd_tot += 16
    K.pool.eng.dma_start(out=out, in_=in_).then_inc(K.dd_sem, 16)
    K.latest[id(K.dd_sem)] = (K.dd_sem, K.dd_tot)


def prologue(K, depth):
    import os
    parts = os.environ.get("PRO_PARTS", "123")
    dr = K.dr
    nc = K.nc
    for L in range(depth if '1' in parts else 0):
        for r in range(0, D, 256):
            dma_dd(K, dr.w_in_b[L, r:r + 256, 0:IN_DIM], dr.w_in[L, r:r + 256, :])
        for cg in range(4):
            dma_dd(K, dr.w_out_b[L, cg], dr.w_out[L, :, cg * 512:(cg + 1) * 512].rearrange("(kt p) c -> p kt c", p=128))
        for cg in range(16):
            dma_dd(K, dr.w_ff1_b[L, cg], dr.w_ff1[L, :, cg * 512:(cg + 1) * 512].rearrange("(kt p) c -> p kt c", p=128))
        for cg in range(4):
            for kb in range(4):
                dma_dd(K, dr.w_ff2_b[L, cg * 4 + kb],
                       dr.w_ff2[L, kb * 2048:(kb + 1) * 2048, cg * 512:(cg + 1) * 512].rearrange("(kt p) c -> p kt c", p=128))
        src = dr.w_kv_up[L].rearrange("k (h c) -> k h c", h=6)
        dma_dd(K, dr.w_kv_b[L, :, 0:768].rearrange("k (h c) -> k h c", h=6), src[:, :, 0:128])
        dma_dd(K, dr.w_kv_b[L, :, 768:1536].rearrange("k (h c) -> k h c", h=6), src[:, :, 128:256])
    with ExitStack() as es:
        wq = K.sb(es, "p_wq", [128, 4, 1152], F32)
        wqo = K.sb(es, "p_wqo", [128, 4, 1536], BF16)
        kp = K.sb(es, "p_kp", [128, 16, 64], F32)
        kpo = K.sb(es, "p_kpo", [128, 16, 64], BF16)
        sc = 1.0 / np.sqrt(QK_HEAD)
        for L in range(depth if '2' in parts else 0):
            K.sp.dma(wq[:], dr.w_q_up[L].rearrange("(kt p) c -> p kt c", p=128), wq, True)
            wq4 = wq[:].rearrange("p k (h c) -> p k h c", h=6)
            K.dve.op(lambda e: e.tensor_scalar_mul(wqo[:, :, 0:768].rearrange("p k (h c) -> p k h c", h=6),
                                                   wq4[:, :, :, 0:128], sc), reads=[wq], writes=[wqo])
            K.dve.op(lambda e: e.tensor_scalar_mul(wqo[:, :, 768:1152].rearrange("p k (h c) -> p k h c", h=6),
                                                   wq4[:, :, :, 128:192], sc), reads=[wq], writes=[wqo])
            rot = wqo[:, :, 1152:1536].rearrange("p k (h c) -> p k h c", h=6)
            K.dve.op(lambda e: e.tensor_scalar_mul(rot[:, :, :, 0:32], wq4[:, :, :, 160:192], -sc), reads=[wq], writes=[wqo])
            K.dve.op(lambda e: e.tensor_scalar_mul(rot[:, :, :, 32:64], wq4[:, :, :, 128:160], sc), reads=[wq], writes=[wqo])
            K.pool.dma(dr.w_q_b[L].rearrange("(kt p) c -> p kt c", p=128), wqo[:], wqo, False)
            K.sp.dma(kp[:], dr.w_in[L, :, C_KPE:C_KPE + 64].rearrange("(kt p) c -> p kt c", p=128), kp, True)
            K.dve.op(lambda e: e.tensor_scalar_mul(kpo[:, :, 0:32], kp[:, :, 32:64], -1.0), reads=[kp], writes=[kpo])
            K.dve.op(lambda e: e.tensor_copy(kpo[:, :, 32:64], kp[:, :, 0:32]), reads=[kp], writes=[kpo])
            K.pool.dma(dr.w_in_b[L, :, IN_DIM:WIN_COLS].rearrange("(kt p) c -> p kt c", p=128), kpo[:], kpo, False)
        K.barrier()
    with ExitStack() as es:
        idf = K.sb(es, "p_idf", [128, 128], F32)
        K.sp.dma(idf[:], dr.c["ident_f"], idf, True)
        xin = [K.sb(es, "p_xin%d" % i, [128, D], F32) for i in range(2)]
        xo = [K.sb(es, "p_xo%d" % i, [128, NKT, TT], F32) for i in range(2)]
        nt = K.T // TT
        pi = 0
        for ti in range(nt if '3' in parts else 0):
            o = xo[ti % 2]
            for s4 in range(4):
                r0 = ti * TT + s4 * 128
                xi = xin[(ti * 4 + s4) % 2]
                K.sp.dma(xi[:], dr.x[r0:r0 + 128, :], xi, True)
                for kq in range(4):
                    ps = K.psum[pi % 8]
                    pi += 1
                    for k4 in range(4):
                        kt = kq * 4 + k4
                        K.pe.op(lambda e: e.transpose(ps[:, k4 * 128:(k4 + 1) * 128], xi[:, kt * 128:(kt + 1) * 128], idf[:]),
                                reads=[xi, idf], writes=[ps], inc=(k4 == 3))
                    ev = K.ev()
                    src = ps[:, :].rearrange("p (k t) -> p k t", k=4)
                    dst = o[:, kq * 4:(kq + 1) * 4, s4 * 128:(s4 + 1) * 128]
                    if ev is K.dve:
                        ev.op(lambda e: e.tensor_copy(dst, src), reads=[ps], writes=[o])
                    else:
                        ev.op(lambda e: e.activation(dst, src, AF.Copy), reads=[ps], writes=[o])
            K.pool.dma(dr.xT[:, ti * TT:(ti + 1) * TT].rearrange("(kt p) t -> p kt t", p=128), o[:], o, False)


def epilogue(K):
    dr = K.dr
    with ExitStack() as es:
        idf = K.sb(es, "e_idf", [128, 128], F32)
        K.sp.dma(idf[:], dr.c["ident_f"], idf, True)
        xin = [K.sb(es, "e_xin%d" % i, [128, NKT, TT], F32) for i in range(2)]
        yo = [K.sb(es, "e_yo%d" % i, [128, D], F32) for i in range(2)]
        nt = K.T // TT
        pi = 0
        for ti in range(nt):
            xi = xin[ti % 2]
            K.sp.dma(xi[:], dr.xT[:, ti * TT:(ti + 1) * TT].rearrange("(kt p) t -> p kt t", p=128), xi, True)
            for s4 in range(4):
                o = yo[(ti * 4 + s4) % 2]
                for kq in range(4):
                    ps = K.psum[pi % 8]
                    pi += 1
                    for k4 in range(4):
                        kt = kq * 4 + k4
                        K.pe.op(lambda e: e.transpose(ps[:, k4 * 128:(k4 + 1) * 128], xi[:, kt, s4 * 128:(s4 + 1) * 128], idf[:]),
                                reads=[xi, idf], writes=[ps], inc=(k4 == 3))
                    ev = K.ev()
                    dst = o[:, kq * 512:(kq + 1) * 512]
                    if ev is K.dve:
                        ev.op(lambda e: e.tensor_copy(dst, ps[:, :]), reads=[ps], writes=[o])
                    else:
                        ev.op(lambda e: e.activation(dst, ps[:, :], AF.Copy), reads=[ps], writes=[o])
                r0 = ti * TT + s4 * 128
                K.pool.dma(dr.y[r0:r0 + 128, :], o[:], o, False)


def evac(K, dst, dst_buf, ps, src, scale_ap=None, extra_reads=()):
    ev = K.ev()
    rd = [ps] + list(extra_reads)
    if ev is K.dve:
        if scale_ap is None:
            ev.op(lambda e: e.tensor_copy(dst, src), reads=rd, writes=[dst_buf])
        else:
            ev.op(lambda e: e.tensor_scalar_mul(dst, src, scale_ap), reads=rd, writes=[dst_buf])
    else:
        if scale_ap is None:
            ev.op(lambda e: e.activation(dst, src, AF.Copy), reads=rd, writes=[dst_buf])
        else:
            ev.op(lambda e: e.activation(dst, src, AF.Copy, scale=scale_ap), reads=rd, writes=[dst_buf])


class Rot:
    def __init__(self, items):
        self.items = items
        self.i = 0

    def next(self):
        x = self.items[self.i % len(self.items)]
        self.i += 1
        return x


def rms_rep(K, ps, src_fn, nk, ones, sqs, src_bufs, rstd, n=TT):
    for kt in range(nk):
        sq = sqs.next()
        K.act.op(lambda e: e.activation(sq[:, 0:n], src_fn(kt), AF.Square), reads=src_bufs, writes=[sq])
        K.pe.op(lambda e: e.matmul(ps[:, 0:n], ones[:], sq[:, 0:n], start=(kt == 0), stop=(kt == nk - 1)),
                reads=[sq, ones], writes=[ps], inc=(kt == nk - 1))
    K.act.op(lambda e: e.activation(rstd[:, 0:n], ps[:, 0:n], AF.Sqrt, bias=EPS, scale=1.0), reads=[ps], writes=[rstd])
    K.dve.op(lambda e: e.reciprocal(rstd[:, 0:n], rstd[:, 0:n]), reads=[rstd], writes=[rstd])


def phase1(K, L):
    import os
    P1_STOP = int(os.environ.get('P1_STOP', '99'))
    _only = os.environ.get('P1_ONLY', '')
    P1_ON = lambda n: (str(n) in _only) if _only else (n <= P1_STOP)
    dr = K.dr
    with ExitStack() as es:
        xt = K.sb(es, "a_xt", [128, NKT, TT], F32)
        hT = K.sb(es, "a_hT", [128, NKT, TT], BF16)
        sqs = Rot([K.sb(es, "a_sq%d" % i, [128, TT], BF16) for i in range(3)])
        wbs = Rot([K.sb(es, "a_wb%d" % i, [128, NKT, 512], BF16) for i in range(2)])
        rstd = K.sb(es, "a_rstd", [128, TT], F32)
        rq = K.sb(es, "a_rq", [128, TT], F32)
        rkv = K.sb(es, "a_rkv", [128, TT], F32)
        cq = K.sb(es, "a_cq", [128, 4, TT], F32)
        ckv = K.sb(es, "a_ckv", [128, 4, TT], F32)
        cqn = K.sb(es, "a_cqn", [128, 4, TT], BF16)
        ckvn = K.sb(es, "a_ckvn", [128, 4, TT], BF16)
        wq = K.sb(es, "a_wq", [128, 4, 1536], BF16)
        wkv = K.sb(es, "a_wkv", [128, 4, 1536], BF16)
        cosb = K.sb(es, "a_cos", [128, TT], F32)
        sinb = K.sb(es, "a_sin", [128, TT], F32)
        cosb2 = K.sb(es, "a_cos2", [128, TT], F32)
        sinb2 = K.sb(es, "a_sin2", [128, TT], F32)
        onesD = K.sb(es, "a_onesD", [128, 128], BF16)
        ones5 = K.sb(es, "a_ones5", [128, 128], BF16)
        gpre = K.sb(es, "a_gpre", [128, NKT], F32)
        gq = K.sb(es, "a_gq", [128, 4], F32)
        gkv = K.sb(es, "a_gkv", [128, 4], F32)
        stg = Rot([K.sb(es, "a_stg%d" % i, [128, 512], BF16) for i in range(int(os.environ.get("NSTG", "6")))])
        stz = Rot([K.sb(es, "a_stz%d" % i, [128, 768], BF16) for i in range(2)])
        stf = Rot([K.sb(es, "a_stf%d" % i, [128, 64], F32) for i in range(2)])
        t1 = Rot([K.sb(es, "a_t1%d" % i, [128, TT], F32) for i in range(2)])
        t2 = Rot([K.sb(es, "a_t2%d" % i, [128, TT], F32) for i in range(2)])
        psr = Rot(K.psum[2:8])
        ps_st = K.psum[0]
        ps_st2 = K.psum[1]

        K.sp.dma(onesD[:], dr.c["onesD"], onesD, True)
        K.sp.dma(ones5[:], dr.c["ones512"], ones5, True)
        with K.nc.allow_non_contiguous_dma("small gain vectors"):
            K.sp.dma(gpre[:], dr.pre_mix_g[L].rearrange("(kt p) -> p kt", p=128), gpre, True)
            K.sp.dma(gq[:], dr.q_norm_g[L].rearrange("(kt p) -> p kt", p=128), gq, True)
            K.sp.dma(gkv[:], dr.kv_norm_g[L].rearrange("(kt p) -> p kt", p=128), gkv, True)
        K.sp.dma(wq[:], dr.w_q_b[L].rearrange("(kt p) c -> p kt c", p=128), wq, True)
        K.sp.dma(wkv[:], dr.w_kv_b[L].rearrange("(kt p) c -> p kt c", p=128), wkv, True)
        wsrc = dr.w_in_b[L].rearrange("(kt p) c -> p kt c", p=128)

        def load_w(c0, n, dst0=0, wb=None):
            if wb is None:
                wb = wbs.next()
            K.sp.dma(wb[:, :, dst0:dst0 + n], wsrc[:, :, c0:c0 + n], wb, True)
            return wb

        def feat_block(wb, col0, M, rhs_buf, rhs_fn, nk=NKT):
            ps = psr.next()
            for kt in range(nk):
                K.pe.op(lambda e: e.matmul(ps[0:M, :], wb[:, kt, col0:col0 + M], rhs_fn(kt), start=(kt == 0), stop=(kt == nk - 1)),
                        reads=[wb, rhs_buf], writes=[ps], inc=(kt == nk - 1))
            return ps

        def tok_block(wb, col0, n, ts, lhs_buf, lhs_fn, nk=NKT):
            ps = psr.next()
            for kt in range(nk):
                K.pe.op(lambda e: e.matmul(ps[:, 0:n], lhs_fn(kt, ts), wb[:, kt, col0:col0 + n], start=(kt == 0), stop=(kt == nk - 1)),
                        reads=[wb, lhs_buf], writes=[ps], inc=(kt == nk - 1))
            return ps

        h_rhs = lambda kt: hT[:, kt, :]
        h_lhs = lambda kt, ts: hT[:, kt, ts * 128:(ts + 1) * 128]

        tiles = [(T0, S, j) for (T0, S) in K.seqs for j in range(S // TT)]
        cs_r = Rot([(cosb, sinb), (cosb2, sinb2)])

        def load_x(idx):
            T0_, S_, j_ = tiles[idx]
            ta_ = T0_ + j_ * TT
            cb_, sb_ = cs_r.next()
            K.sp.dma(xt[:], dr.xT[:, ta_:ta_ + TT].rearrange("(kt p) t -> p kt t", p=128), xt, True)
            K.sp.dma(cb_[:], dr.c["cos128"][:, j_ * TT:(j_ + 1) * TT], cb_, True)
            K.sp.dma(sb_[:], dr.c["sin128"][:, j_ * TT:(j_ + 1) * TT], sb_, True)
            return cb_, sb_

        nxt = load_x(0)
        for ti, (T0, S, j) in enumerate(tiles):
            if True:
                ta = T0 + j * TT
                cosb, sinb = nxt
                rms_rep(K, ps_st, lambda kt: xt[:, kt, :], NKT, onesD, sqs, [xt], rstd)
                for kt in range(NKT):
                    K.dve.op(lambda e: e.scalar_tensor_tensor(hT[:, kt, :], xt[:, kt, :], gpre[:, kt:kt + 1], rstd[:], op0=ALU.mult, op1=ALU.mult),
                             reads=[xt, gpre, rstd], writes=[hT])
                if ti + 1 < len(tiles):
                    nxt = load_x(ti + 1)
                if P1_ON(1):
                    wb = load_w(C_CQ, 512)
                    for m in range(4):
                        ps = feat_block(wb, m * 128, 128, hT, h_rhs)
                        evac(K, cq[:, m, :], cq, ps, ps[:, :])
                    wb = load_w(C_CKV, 512)
                    for m in range(4):
                        ps = feat_block(wb, m * 128, 128, hT, h_rhs)
                        evac(K, ckv[:, m, :], ckv, ps, ps[:, :])
                if P1_ON(2):
                    rms_rep(K, ps_st2, lambda kt: cq[:, kt, :], 4, ones5, sqs, [cq], rq)
                    rms_rep(K, ps_st, lambda kt: ckv[:, kt, :], 4, ones5, sqs, [ckv], rkv)
                    for kt in range(4):
                        K.dve.op(lambda e: e.scalar_tensor_tensor(cqn[:, kt, :], cq[:, kt, :], gq[:, kt:kt + 1], rq[:], op0=ALU.mult, op1=ALU.mult),
                                 reads=[cq, gq, rq], writes=[cqn])
                        K.dve.op(lambda e: e.scalar_tensor_tensor(ckvn[:, kt, :], ckv[:, kt, :], gkv[:, kt:kt + 1], rkv[:], op0=ALU.mult, op1=ALU.mult),
                                 reads=[ckv, gkv, rkv], writes=[ckvn])
                if P1_ON(3):
                    wb = load_w(C_KPE, 64)
                    load_w(IN_DIM, 64, dst0=64, wb=wb)
                    psa = feat_block(wb, 0, 64, hT, h_rhs)
                    psb = feat_block(wb, 64, 64, hT, h_rhs)
                    a1 = t1.next(); a2 = t2.next(); so = stg.next()
                    K.dve.op(lambda e: e.tensor_tensor(a1[0:64, :], psa[0:64, :], cosb[0:64, :], op=ALU.mult), reads=[psa, cosb], writes=[a1])
                    K.dve.op(lambda e: e.tensor_tensor(a2[0:64, :], psb[0:64, :], sinb[0:64, :], op=ALU.mult), reads=[psb, sinb], writes=[a2])
                    K.dve.op(lambda e: e.tensor_tensor(so[0:64, :], a1[0:64, :], a2[0:64, :], op=ALU.add), reads=[a1, a2], writes=[so])
                    K.pool.dma(dr.kpeT[:, ta:ta + TT], so[0:64, :], so, False)
                if P1_ON(4):
                    c_rhs = lambda kt: cqn[:, kt, :]
                    for h in range(6):
                        ps = feat_block(wq, h * 128, 128, cqn, c_rhs, nk=4)
                        so = stg.next()
                        evac(K, so[:, :], so, ps, ps[:, :])
                        K.pool.dma(dr.qT[h, 0:128, ta:ta + TT], so[:, :], so, False)
                    for hp in range(3):
                        psa = feat_block(wq, 768 + hp * 128, 128, cqn, c_rhs, nk=4)
                        psb = feat_block(wq, 1152 + hp * 128, 128, cqn, c_rhs, nk=4)
                        a1 = t1.next(); a2 = t2.next(); so = stg.next()
                        K.dve.op(lambda e: e.tensor_tensor(a1[:, :], psa[:, :], cosb[:, :], op=ALU.mult), reads=[psa, cosb], writes=[a1])
                        K.dve.op(lambda e: e.tensor_tensor(a2[:, :], psb[:, :], sinb[:, :], op=ALU.mult), reads=[psb, sinb], writes=[a2])
                        K.dve.op(lambda e: e.tensor_tensor(so[:, :], a1[:, :], a2[:, :], op=ALU.add), reads=[a1, a2], writes=[so])
                        K.pool.dma(dr.qT[2 * hp, 128:192, ta:ta + TT], so[0:64, :], so, False)
                        K.pool.dma(dr.qT[2 * hp + 1, 128:192, ta:ta + TT], so[64:128, :], so, False)
                if P1_ON(5):
                    kv_rhs = lambda kt: ckvn[:, kt, :]
                    for h in range(6):
                        ps = feat_block(wkv, h * 128, 128, ckvn, kv_rhs, nk=4)
                        so = stg.next()
                        evac(K, so[:, :], so, ps, ps[:, :])
                        K.pool.dma(dr.kT[h, :, ta:ta + TT], so[:, :], so, False)
                    kv_lhs = lambda kt, ts: ckvn[:, kt, ts * 128:(ts + 1) * 128]
                    for ts in range(4):
                        sz = stz.next()
                        ps = tok_block(wkv, 768, 512, ts, ckvn, kv_lhs, nk=4)
                        evac(K, sz[:, 0:512], sz, ps, ps[:, :])
                        ps = tok_block(wkv, 768 + 512, 256, ts, ckvn, kv_lhs, nk=4)
                        evac(K, sz[:, 512:768], sz, ps, ps[:, 0:256])
                        K.pool.dma(dr.v_tok[ta + ts * 128:ta + (ts + 1) * 128, :], sz[:, :], sz, False)
                if P1_ON(6):
                    wb = load_w(C_U, 512)
                    for ts in range(4):
                        ps = tok_block(wb, 0, 512, ts, hT, h_lhs)
                        so = stg.next()
                        evac(K, so[:, :], so, ps, ps[:, :])
                        r0 = ta + ts * 128
                        K.pool.dma(dr.u_d[:, r0:r0 + 128, :].rearrange("g t c -> t g c"), so[:, :].rearrange("p (g c) -> p g c", g=4), so, False)
                if P1_ON(7):
                    P1_VAR = int(os.environ.get('P1_VAR', '0'))
                    wb = load_w(C_Z, 512)
                    wb2 = load_w(C_Z + 512, 256)
                    if P1_VAR not in (1, 5):
                        load_w(C_DT, 24, dst0=256, wb=wb2)
                        K.dve.op(lambda e: e.memset(wb2[:, :, 280:320], 0.0), writes=[wb2])
                    for ts in range(4):
                        sz = stz.next()
                        ps = tok_block(wb, 0, 512, ts, hT, h_lhs)
                        evac(K, sz[:, 0:512], sz, ps, ps[:, :])
                        ps = tok_block(wb2, 0, 256 if P1_VAR == 1 else 320, ts, hT, h_lhs)
                        if P1_VAR == 6:
                            K.dve.op(lambda e: e.tensor_copy(sz[:, 512:768], ps[:, 0:256]), reads=[ps], writes=[sz])
                        else:
                            evac(K, sz[:, 512:768], sz, ps, ps[:, 0:256])
                        r0 = ta + ts * 128
                        K.pool.dma(dr.z_tok[r0:r0 + 128, :], sz[:, :], sz, False)
                        if P1_VAR == 1:
                            continue
                        sf = stf.next()
                        K.dve.op(lambda e: e.tensor_copy(sf[:, :], ps[:, 256:320]), reads=[ps], writes=[sf])
                        if P1_VAR != 2:
                            if P1_VAR == 3:
                                K.pool.dma(dr.yf[r0:r0 + 128, 0:64], sf[:, :], sf, False)
                            elif P1_VAR == 4:
                                K.pool.dma(dr.dt_tok[r0:r0 + 128, :], t1.items[0][:, 0:64], t1.items[0], False)
                            else:
                                K.pool.dma(dr.dt_tok[r0:r0 + 128, :], sf[:, :], sf, False)
                if P1_ON(8):
                    for (c0, n) in ((C_XBC, 512), (C_XBC + 512, 512), (C_XBC + 1024, 256)):
                        wb = load_w(c0, n)
                        for m in range(n // 128):
                            ps = feat_block(wb, m * 128, 128, hT, h_rhs)
                            so = stg.next()
                            evac(K, so[:, :], so, ps, ps[:, :])
                            f0 = (c0 - C_XBC) + m * 128
                            K.pool.dma(dr.xbcT[f0:f0 + 128, ta:ta + TT], so[:, :], so, False)


def phase_fnet(K, L):
    dr = K.dr
    with ExitStack() as es:
        xg = K.sb(es, "f_xg", [128, 128, 128], BF16)
        bre = K.sb(es, "f_bre", [128, 128 * 128], BF16)
        bim = K.sb(es, "f_bim", [128, 128 * 128], BF16)
        pst = Rot([K.sb(es, "f_pst%d" % i, [128, 2048], BF16) for i in range(2)])
        qst = Rot([K.sb(es, "f_qst%d" % i, [128, 2048], BF16) for i in range(2)])
        dft = K.sb(es, "f_dft", [128, 256], BF16)
        twc = K.sb(es, "f_twc", [128, 128], F32)
        tws = K.sb(es, "f_tws", [128, 128], F32)
        c128 = K.sb(es, "f_c128", [128, 128], BF16)
        s128 = K.sb(es, "f_s128", [128, 128], BF16)
        ns128 = K.sb(es, "f_ns128", [128, 128], BF16)
        tt = [Rot([K.sb(es, "f_t%d_%d" % (q, i), [128, 512], F32) for i in range(2)]) for q in range(4)]
        K.sp.dma(c128[:], dr.c["C128"], c128, True)
        K.sp.dma(s128[:], dr.c["S128"], s128, True)
        K.sp.dma(ns128[:], dr.c["nS128"], ns128, True)
        psr = Rot(K.psum[0:4])
        psP = Rot(K.psum[4:6])
        psQ = Rot(K.psum[6:8])
        for (T0, S) in K.seqs:
            na = S // 128
            nch = 512 // (2 * na)
            if nch > 128:
                nch = 128
            K.sp.dma(dft[:, 0:2 * na], dr.c["dftA_%d" % S], dft, True)
            K.sp.dma(twc[:, 0:na], dr.c["twc_%d" % S], twc, True)
            K.sp.dma(tws[:, 0:na], dr.c["tws_%d" % S], tws, True)
            for g in range(4):
                K.sp.dma(xg[0:na, :, :], dr.u_d[g, T0:T0 + S, :].rearrange("(a b) c -> a b c", b=128), xg, True)
                bre3 = bre[:, 0:128 * na].rearrange("p (c k) -> p c k", k=na)
                bim3 = bim[:, 0:128 * na].rearrange("p (c k) -> p c k", k=na)
                for c0 in range(0, 128, nch):
                    ps = psr.next()
                    for ci in range(nch):
                        c = c0 + ci
                        K.pe.op(lambda e: e.matmul(ps[:, ci * 2 * na:(ci + 1) * 2 * na], xg[0:na, :, c], dft[0:na, 0:2 * na], start=True, stop=True),
                                reads=[xg, dft], writes=[ps])
                    p4 = ps[:, 0:nch * 2 * na].rearrange("p (c r k) -> p c r k", r=2, k=na)
                    ac = p4[:, :, 0, :]
                    as_ = p4[:, :, 1, :]
                    tcb = twc[:, 0:na].unsqueeze(1).broadcast_to([128, nch, na])
                    tsb = tws[:, 0:na].unsqueeze(1).broadcast_to([128, nch, na])
                    t = [tt[q].next() for q in range(4)]
                    tv = [x[:, 0:nch * na].rearrange("p (c k) -> p c k", k=na) for x in t]
                    K.dve.op(lambda e: e.tensor_tensor(tv[0], ac, tcb, op=ALU.mult), reads=[ps, twc], writes=[t[0]])
                    K.dve.op(lambda e: e.tensor_tensor(tv[1], as_, tsb, op=ALU.mult), reads=[ps, tws], writes=[t[1]])
                    K.dve.op(lambda e: e.tensor_tensor(tv[2], ac, tsb, op=ALU.mult), reads=[ps, tws], writes=[t[2]])
                    K.dve.op(lambda e: e.tensor_tensor(tv[3], as_, tcb, op=ALU.mult), reads=[ps, twc], writes=[t[3]])
                    K.pool.op(lambda e: e.tensor_tensor(bre3[:, c0:c0 + nch, :], tv[0], tv[1], op=ALU.subtract), reads=[t[0], t[1]], writes=[bre])
                    K.pool.op(lambda e: e.tensor_tensor(bim3[:, c0:c0 + nch, :], tv[2], tv[3], op=ALU.add), reads=[t[2], t[3]], writes=[bim])
                ncol = 128 * na
                CH = min(2048, 64 * na, ncol)
                for s0 in range(0, ncol, CH):
                    ps_t = pst.next(); qs_t = qst.next()
                    for q0 in range(0, CH, 512):
                        n = min(512, CH - q0)
                        a0 = s0 + q0
                        pp = psP.next(); pq = psQ.next()
                        K.pe.op(lambda e: e.matmul(pp[:, 0:n], c128[:, :], bre[:, a0:a0 + n], start=True, stop=False), reads=[c128, bre], writes=[pp])
                        K.pe.op(lambda e: e.matmul(pp[:, 0:n], ns128[:, :], bim[:, a0:a0 + n], start=False, stop=True), reads=[ns128, bim], writes=[pp])
                        K.pe.op(lambda e: e.matmul(pq[:, 0:n], c128[:, :], bim[:, a0:a0 + n], start=True, stop=False), reads=[c128, bim], writes=[pq])
                        K.pe.op(lambda e: e.matmul(pq[:, 0:n], s128[:, :], bre[:, a0:a0 + n], start=False, stop=True), reads=[s128, bre], writes=[pq])
                        evac(K, ps_t[:, q0:q0 + n], ps_t, pp, pp[:, 0:n])
                        evac(K, qs_t[:, q0:q0 + n], qs_t, pq, pq[:, 0:n])
                    cc0 = s0 // na
                    ncc = CH // na
                    with K.nc.allow_non_contiguous_dma("fnet relayout"):
                        K.pool.dma(dr.pT[g * 128 + cc0:g * 128 + cc0 + ncc, T0:T0 + S].rearrange("c (k2 k1) -> k2 c k1", k1=na),
                                   ps_t[:, 0:CH].rearrange("p (c k) -> p c k", k=na), ps_t, False)
                        K.pool.dma(dr.qqT[g * 128 + cc0:g * 128 + cc0 + ncc, T0:T0 + S].rearrange("c (k2 k1) -> k2 c k1", k1=na),
                                   qs_t[:, 0:CH].rearrange("p (c k) -> p c k", k=na), qs_t, False)
    K.barrier()
    with ExitStack() as es:
        chc = K.sb(es, "f_chc", [128, 128], BF16)
        chs = K.sb(es, "f_chs", [128, 128], BF16)
        pin = Rot([K.sb(es, "f_pin%d" % i, [128, TT], BF16) for i in range(3)])
        qin = Rot([K.sb(es, "f_qin%d" % i, [128, TT], BF16) for i in range(3)])
        fo = Rot([K.sb(es, "f_fo%d" % i, [128, TT], BF16) for i in range(3)])
        psr = Rot(K.psum[0:8])
        for (T0, S) in K.seqs:
            K.sp.dma(chc[:], dr.c["chC_%d" % S], chc, True)
            K.sp.dma(chs[:], dr.c["chnS_%d" % S], chs, True)
            for j in range(S // TT):
                ta = T0 + j * TT
                for g in range(4):
                    p = pin.next(); q = qin.next(); o = fo.next(); ps = psr.next()
                    K.sp.dma(p[:], dr.pT[g * 128:(g + 1) * 128, ta:ta + TT], p, True)
                    K.sp.dma(q[:], dr.qqT[g * 128:(g + 1) * 128, ta:ta + TT], q, True)
                    K.pe.op(lambda e: e.matmul(ps[:, :], chc[:, :], p[:, :], start=True, stop=False), reads=[chc, p], writes=[ps])
                    K.pe.op(lambda e: e.matmul(ps[:, :], chs[:, :], q[:, :], start=False, stop=True), reads=[chs, q], writes=[ps])
                    evac(K, o[:, :], o, ps, ps[:, :])
                    K.pool.dma(dr.mixT[g * 128:(g + 1) * 128, ta:ta + TT], o[:, :], o, False)


def phase_ssd(K, L):
    dr = K.dr
    with ExitStack() as es:
        cw = K.sb(es, "s_cw", [128, 10, 5], F32)
        cb = K.sb(es, "s_cb", [128, 10], F32)
        dtb = K.sb(es, "s_dtb", [128, 24], F32)
        arep = K.sb(es, "s_arep", [128, 24], F32)
        drep = K.sb(es, "s_drep", [128, 12], F32)
        gss = K.sb(es, "s_gss", [128, 6], F32)
        msk = {}
        for nm in ("Uf", "Ub", "Tf", "Tb", "Mf", "Mb", "ones_f"):
            msk[nm] = K.sb(es, "s_" + nm, [128, 128], F32)
            K.sp.dma(msk[nm][:], dr.c[nm], msk[nm], True)
        idb = K.sb(es, "s_idb", [128, 128], BF16)
        K.sp.dma(idb[:], dr.c["ident_b"], idb, True)
        with K.nc.allow_non_contiguous_dma("small per-channel vectors"):
            for jj in range(5):
                K.sp.dma(cw[:, :, jj], dr.conv_w[L, jj].rearrange("(t p) -> p t", p=128), cw, True)
            K.sp.dma(cb[:], dr.conv_b[L].rearrange("(t p) -> p t", p=128), cb, True)
            K.sp.dma(gss[:], dr.ssd_norm_g[L].rearrange("(t p) -> p t", p=128), gss, True)
            K.sp.dma(dtb[:], dr.dt_bias[L:L + 1, :].partition_broadcast(128), dtb, True)
            K.sp.dma(arep[:], dr.a_log[L:L + 1, :].partition_broadcast(128), arep, True)
            K.sp.dma(drep[:], dr.d_skip[L:L + 1, :].partition_broadcast(128), drep, True)
        K.act.op(lambda e: e.activation(arep[:], arep[:], AF.Exp), reads=[arep], writes=[arep])
        K.dve.op(lambda e: e.tensor_scalar_mul(arep[:], arep[:], -1.0), reads=[arep], writes=[arep])

        xh_r = Rot([K.sb(es, "s_xh%d" % i, [128, 10, TT + 4], BF16) for i in range(2)])
        cv_r = Rot([K.sb(es, "s_cv%d" % i, [128, 10, TT], BF16) for i in range(2)])
        accs = Rot([K.sb(es, "s_acc%d" % i, [128, TT], F32) for i in range(2)])
        ptmp = K.sb(es, "s_ptmp", [128, TT], F32)
        dtin4_r = Rot([K.sb(es, "s_dtin4%d" % i, [128, 4, 64], F32) for i in range(2)])
        dtr4_r = Rot([K.sb(es, "s_dtr4%d" % i, [128, 4, 24], F32) for i in range(2)])
        sa4_r = Rot([K.sb(es, "s_sa4%d" % i, [128, 4, 24], F32) for i in range(2)])
        sb4_r = Rot([K.sb(es, "s_sb4%d" % i, [128, 4, 24], F32) for i in range(2)])
        dt4_r = Rot([K.sb(es, "s_dt4%d" % i, [128, 4, 24], F32) for i in range(2)])
        ad4_r = Rot([K.sb(es, "s_ad4%d" % i, [128, 4, 24], F32) for i in range(2)])
        xs_tok_r = Rot([K.sb(es, "s_xs%d" % i, [128, 768], BF16) for i in range(2)])
        b_tok_r = Rot([K.sb(es, "s_bt%d" % i, [128, 256], BF16) for i in range(2)])
        xd_r = Rot([K.sb(es, "s_xd%d" % i, [128, 768], BF16) for i in range(2)])
        xdd_r = Rot([K.sb(es, "s_xdd%d" % i, [128, 768], BF16) for i in range(2)])
        s1_r = Rot([K.sb(es, "s_s1%d" % i, [128, 24], F32) for i in range(2)])
        s2_r = Rot([K.sb(es, "s_s2%d" % i, [128, 24], F32) for i in range(2)])
        sm_r = Rot([K.sb(es, "s_sm%d" % i, [128, 36], F32) for i in range(2)])
        cbm_r = Rot([K.sb(es, "s_cbm%d" % i, [128, 256], F32) for i in range(2)])
        wt_r = Rot([K.sb(es, "s_wt%d" % i, [128, 768], F32) for i in range(2)])
        ex_r = Rot([K.sb(es, "s_ex%d" % i, [128, 768], F32) for i in range(2)])
        mT_r = Rot([K.sb(es, "s_mT%d" % i, [128, 768], BF16) for i in range(2)])
        st = K.sb(es, "s_st", [128, 768], F32)
        stb = K.sb(es, "s_stb", [128, 768], BF16)
        tmp_r = Rot([K.sb(es, "s_tmp%d" % i, [128, 384], F32) for i in range(2)])
        tmp2_r = Rot([K.sb(es, "s_tmp2%d" % i, [128, 384], F32) for i in range(2)])
        ych = Rot([K.sb(es, "s_y%d" % i, [128, 768], F32) for i in range(2)])
        yfc_r = Rot([K.sb(es, "s_yfc%d" % i, [128, 768], F32) for i in range(2)])
        zc_r = Rot([K.sb(es, "s_zc%d" % i, [128, 768], BF16) for i in range(2)])
        szc_r = Rot([K.sb(es, "s_szc%d" % i, [128, 768], F32) for i in range(2)])
        sqj_r = Rot([K.sb(es, "s_sqj%d" % i, [128, 384], F32) for i in range(2)])
        ss_r = Rot([K.sb(es, "s_ss%d" % i, [128, 2], F32) for i in range(2)])
        ynb_r = Rot([K.sb(es, "s_ynb%d" % i, [128, 768], BF16) for i in range(2)])
        omix = K.sb(es, "s_omix", [128, 6, TT], BF16)
        ps_tr, ps_m, ps_cb, ps_a, ps_b, ps_y, ps_o, ps_s = K.psum

        def do_tile(T0, S, j, d):
            ta = T0 + j * TT
            last = (j == S // TT - 1)
            cv = cv_r.next()
            if d == 1:
                K.sp.dma(cv[:], dr.cvT[:, ta:ta + TT].rearrange("(t p) s -> p t s", p=128), cv, True)
            else:
                xh = xh_r.next()
                K.dve.op(lambda e: e.memset(xh[:, :, 0:2], 0.0), writes=[xh])
                K.dve.op(lambda e: e.memset(xh[:, :, TT + 2:TT + 4], 0.0), writes=[xh])
                K.sp.dma(xh[:, :, 2:TT + 2], dr.xbcT[:, ta:ta + TT].rearrange("(t p) s -> p t s", p=128), xh, True)
                with K.nc.allow_non_contiguous_dma("conv halo"):
                    if j > 0:
                        K.sp.dma(xh[:, :, 0:2], dr.xbcT[:, ta - 2:ta].rearrange("(t p) s -> p t s", p=128), xh, True)
                    if not last:
                        K.sp.dma(xh[:, :, TT + 2:TT + 4], dr.xbcT[:, ta + TT:ta + TT + 2].rearrange("(t p) s -> p t s", p=128), xh, True)
                for t in range(10):
                    acc = accs.next()
                    K.dve.op(lambda e: e.tensor_scalar_mul(acc[:, :], xh[:, t, 0:TT], cw[:, t, 0:1]), reads=[xh, cw], writes=[acc])
                    for jj in range(1, 5):
                        K.dve.op(lambda e: e.scalar_tensor_tensor(acc[:, :], xh[:, t, jj:jj + TT], cw[:, t, jj:jj + 1], acc[:, :], op0=ALU.mult, op1=ALU.add),
                                 reads=[xh, cw, acc], writes=[acc])
                    K.act.op(lambda e: e.activation(cv[:, t, :], acc[:, :], AF.Silu, bias=cb[:, t:t + 1], scale=1.0), reads=[acc, cb], writes=[cv])
                K.pool.dma(dr.cvT[:, ta:ta + TT].rearrange("(t p) s -> p t s", p=128), cv[:], cv, False)
            dtin4 = dtin4_r.next(); dtr4 = dtr4_r.next(); sa4 = sa4_r.next(); sb4 = sb4_r.next(); dt4 = dt4_r.next(); ad4 = ad4_r.next()
            K.sp.dma(dtin4[:], dr.dt_tok[ta:ta + TT, :].rearrange("(c p) f -> p c f", p=128), dtin4, True)
            K.dve.op(lambda e: e.tensor_tensor(dtr4[:], dtin4[:, :, 0:24], dtb[:, :].unsqueeze(1).broadcast_to([128, 4, 24]), op=ALU.add), reads=[dtin4, dtb], writes=[dtr4])
            K.dve.op(lambda e: e.tensor_scalar_mul(sa4[:], dtr4[:], -1.0), reads=[dtr4], writes=[sa4])
            K.dve.op(lambda e: e.tensor_tensor(sa4[:], sa4[:], dtr4[:], op=ALU.min), reads=[sa4, dtr4], writes=[sa4])
            K.act.op(lambda e: e.activation(sa4[:], sa4[:], AF.Exp), reads=[sa4], writes=[sa4])
            K.act.op(lambda e: e.activation(sa4[:], sa4[:], AF.Ln, bias=1.0, scale=1.0), reads=[sa4], writes=[sa4])
            K.dve.op(lambda e: e.tensor_scalar_max(sb4[:], dtr4[:], 0.0), reads=[dtr4], writes=[sb4])
            K.dve.op(lambda e: e.tensor_tensor(dt4[:], sa4[:], sb4[:], op=ALU.add), reads=[sa4, sb4], writes=[dt4])
            K.dve.op(lambda e: e.tensor_tensor(ad4[:], dt4[:], arep[:, :].unsqueeze(1).broadcast_to([128, 4, 24]), op=ALU.mult), reads=[dt4, arep], writes=[ad4])
            order = range(4) if d == 0 else range(3, -1, -1)
            U = msk["Uf"] if d == 0 else msk["Ub"]
            Tm = msk["Tf"] if d == 0 else msk["Tb"]
            Mm = msk["Mf"] if d == 0 else msk["Mb"]
            for c4 in order:
                cs = slice(c4 * 128, (c4 + 1) * 128)
                xs_tok = xs_tok_r.next()
                b_tok = b_tok_r.next()
                xd = xd_r.next()
                xdd = xdd_r.next()
                s1 = s1_r.next()
                s2 = s2_r.next()
                sm = sm_r.next()
                cbm = cbm_r.next()
                yfc = yfc_r.next()
                zc = zc_r.next()
                szc = szc_r.next()
                sqj = sqj_r.next()
                ss = ss_r.next()
                ynb = ynb_r.next()
                tc0 = ta + c4 * 128
                trb = ps_tr[:, :].bitcast(BF16)
                for t in range(8):
                    K.pe.op(lambda e: e.transpose(trb[:, t * 128:(t + 1) * 128], cv[:, t, cs], idb[:]), reads=[cv, idb], writes=[ps_tr])
                K.act.op(lambda e: e.activation(xs_tok[:, :], trb[:, 0:768], AF.Copy), reads=[ps_tr], writes=[xs_tok])
                K.act.op(lambda e: e.activation(b_tok[:, :], trb[:, 768:1024], AF.Copy), reads=[ps_tr], writes=[b_tok])
                dcol = slice(d * 12, (d + 1) * 12)
                K.dve.op(lambda e: e.tensor_tensor(xd[:, :].rearrange("p (h q) -> p h q", q=64), trb[:, 0:768].rearrange("p (h q) -> p h q", q=64),
                                                   dt4[:, c4, dcol].unsqueeze(2).broadcast_to([128, 12, 64]), op=ALU.mult), reads=[ps_tr, dt4], writes=[xd])
                K.pe.op(lambda e: e.matmul(ps_m[:, 0:12], U[:, :], ad4[:, c4, dcol], start=True, stop=True), reads=[U, ad4], writes=[ps_m])
                K.pe.op(lambda e: e.matmul(ps_m[:, 12:24], Tm[:, :], ad4[:, c4, dcol], start=True, stop=True), reads=[Tm, ad4], writes=[ps_m])
                K.pe.op(lambda e: e.matmul(ps_m[:, 24:36], msk["ones_f"][:, :], ad4[:, c4, dcol], start=True, stop=True), reads=[msk["ones_f"], ad4], writes=[ps_m])
                K.act.op(lambda e: e.activation(sm[:, :], ps_m[:, 0:36], AF.Exp), reads=[ps_m], writes=[sm])
                K.dve.op(lambda e: e.tensor_tensor(xdd[:, :].rearrange("p (h q) -> p h q", q=64), xd[:, :].rearrange("p (h q) -> p h q", q=64),
                                                   sm[:, 0:12].unsqueeze(2).broadcast_to([128, 12, 64]), op=ALU.mult), reads=[xd, sm], writes=[xdd])
                for g in range(2):
                    K.pe.op(lambda e: e.matmul(ps_cb[:, g * 128:(g + 1) * 128], cv[:, 6 + g, cs], cv[:, 8 + g, cs], start=True, stop=True),
                            reads=[cv], writes=[ps_cb])
                K.dve.op(lambda e: e.tensor_tensor(cbm[:, :].rearrange("p (g l) -> p g l", g=2), ps_cb[:, 0:256].rearrange("p (g l) -> p g l", g=2),
                                                   Mm[:, :].unsqueeze(1).broadcast_to([128, 2, 128]), op=ALU.mult), reads=[ps_cb, Mm], writes=[cbm])
                y = ych.next()
                for g in range(2):
                    hc = slice(d * 12 + g * 6, d * 12 + g * 6 + 6)
                    wt = wt_r.next()
                    ex = ex_r.next()
                    mT = mT_r.next()
                    tmp = tmp_r.next()
                    tmp2 = tmp2_r.next()
                    gc = slice(g * 384, (g + 1) * 384)
                    K.pool.op(lambda e: e.tensor_tensor(wt[:, :].rearrange("p (r l) -> p r l", r=6), ad4[:, c4, hc].unsqueeze(2).broadcast_to([128, 6, 128]),
                                                        Tm[:, :].unsqueeze(1).broadcast_to([128, 6, 128]), op=ALU.mult), reads=[ad4, Tm], writes=[wt])
                    K.pe.op(lambda e: e.matmul(ps_a[:, 0:384], U[:, :], wt[:, 0:384], start=True, stop=True), reads=[U, wt], writes=[ps_a])
                    K.pe.op(lambda e: e.matmul(ps_b[:, 0:384], U[:, :], wt[:, 384:768], start=True, stop=True), reads=[U, wt], writes=[ps_b])
                    K.act.op(lambda e: e.activation(ex[:, 0:384], ps_a[:, 0:384], AF.Exp), reads=[ps_a], writes=[ex])
                    K.act.op(lambda e: e.activation(ex[:, 384:768], ps_b[:, 0:384], AF.Exp), reads=[ps_b], writes=[ex])
                    K.dve.op(lambda e: e.tensor_tensor(mT[:, :].rearrange("p (r l) -> p r l", r=6), ex[:, :].rearrange("p (r l) -> p r l", r=6),
                                                       cbm[:, g * 128:(g + 1) * 128].unsqueeze(1).broadcast_to([128, 6, 128]), op=ALU.mult),
                             reads=[ex, cbm], writes=[mT])
                    for r in range(6):
                        K.pe.op(lambda e: e.matmul(ps_y[:, r * 64:(r + 1) * 64], mT[:, r * 128:(r + 1) * 128],
                                                   xd[:, (g * 6 + r) * 64:(g * 6 + r + 1) * 64], start=True, stop=True), reads=[mT, xd], writes=[ps_y])
                    K.pe.op(lambda e: e.matmul(ps_o[:, 0:384], cv[:, 8 + g, cs], stb[:, gc], start=True, stop=True), reads=[cv, stb], writes=[ps_o])
                    K.dve.op(lambda e: e.tensor_tensor(tmp[:, :].rearrange("p (r q) -> p r q", q=64), ps_o[:, 0:384].rearrange("p (r q) -> p r q", q=64),
                                                       sm[:, 12 + g * 6:12 + g * 6 + 6].unsqueeze(2).broadcast_to([128, 6, 64]), op=ALU.mult),
                             reads=[ps_o, sm], writes=[tmp])
                    K.dve.op(lambda e: e.tensor_tensor(y[:, gc], tmp[:, :], ps_y[:, 0:384], op=ALU.add), reads=[tmp, ps_y], writes=[y])
                    K.pe.op(lambda e: e.matmul(ps_s[:, 0:384], b_tok[:, g * 128:(g + 1) * 128], xdd[:, gc], start=True, stop=True),
                            reads=[b_tok, xdd], writes=[ps_s])
                    K.pool.op(lambda e: e.tensor_tensor(tmp2[:, :].rearrange("p (r q) -> p r q", q=64), st[:, gc].rearrange("p (r q) -> p r q", q=64),
                                                        sm[:, 24 + g * 6:24 + g * 6 + 6].unsqueeze(2).broadcast_to([128, 6, 64]), op=ALU.mult),
                              reads=[st, sm], writes=[tmp2])
                    K.dve.op(lambda e: e.tensor_tensor(st[:, gc], tmp2[:, :], ps_s[:, 0:384], op=ALU.add), reads=[tmp2, ps_s], writes=[st])
                    K.act.op(lambda e: e.activation(stb[:, gc], st[:, gc], AF.Copy), reads=[st], writes=[stb])
                if d == 0:
                    K.pool.dma(dr.yf[tc0:tc0 + 128, :], y[:, :], y, False)
                else:
                    K.sp.dma(yfc[:], dr.yf[tc0:tc0 + 128, :], yfc, True)
                    K.sp.dma(zc[:], dr.z_tok[tc0:tc0 + 128, :], zc, True)
                    K.pool.op(lambda e: e.tensor_tensor(y[:, :], y[:, :], yfc[:, :], op=ALU.add), reads=[y, yfc], writes=[y])
                    K.dve.op(lambda e: e.tensor_tensor(szc[:, :].rearrange("p (h q) -> p h q", q=64), xs_tok[:, :].rearrange("p (h q) -> p h q", q=64),
                                                       drep[:, :].unsqueeze(2).broadcast_to([128, 12, 64]), op=ALU.mult), reads=[xs_tok, drep], writes=[szc])
                    K.pool.op(lambda e: e.tensor_tensor(y[:, :], y[:, :], szc[:, :], op=ALU.add), reads=[y, szc], writes=[y])
                    K.act.op(lambda e: e.activation(szc[:, :], zc[:, :], AF.Silu), reads=[zc], writes=[szc])
                    K.dve.op(lambda e: e.tensor_tensor(y[:, :], y[:, :], szc[:, :], op=ALU.mult), reads=[y, szc], writes=[y])
                    for g in range(2):
                        K.act.op(lambda e: e.activation(sqj[:, :], y[:, g * 384:(g + 1) * 384], AF.Square, accum_out=ss[:, g:g + 1]),
                                 reads=[y], writes=[sqj, ss])
                    K.act.op(lambda e: e.activation(ss[:, :], ss[:, :], AF.Sqrt, bias=EPS, scale=1.0 / 384.0), reads=[ss], writes=[ss])
                    K.dve.op(lambda e: e.reciprocal(ss[:, :], ss[:, :]), reads=[ss], writes=[ss])
                    for g in range(2):
                        K.dve.op(lambda e: e.tensor_scalar_mul(ynb[:, g * 384:(g + 1) * 384], y[:, g * 384:(g + 1) * 384], ss[:, g:g + 1]),
                                 reads=[y, ss], writes=[ynb])
                    for t in range(6):
                        K.pe.op(lambda e: e.transpose(trb[:, t * 128:(t + 1) * 128], ynb[:, t * 128:(t + 1) * 128], idb[:]), reads=[ynb, idb], writes=[ps_tr])
                    for t in range(6):
                        K.dve.op(lambda e: e.tensor_scalar_mul(omix[:, t, cs], trb[:, t * 128:(t + 1) * 128], gss[:, t:t + 1]),
                                 reads=[ps_tr, gss], writes=[omix])
            if d == 1:
                K.pool.dma(dr.mixT[1280:2048, ta:ta + TT].rearrange("(t p) s -> p t s", p=128), omix[:], omix, False)

        for (T0, S) in K.seqs:
            for d in (0, 1):
                K.dve.op(lambda e: e.memset(st[:], 0.0), writes=[st])
                K.dve.op(lambda e: e.memset(stb[:], 0.0), writes=[stb])
                if d == 0:
                    K.barrier() if False else None
                tiles = range(S // TT) if d == 0 else range(S // TT - 1, -1, -1)
                for j in tiles:
                    do_tile(T0, S, j, d)
                if d == 0:
                    K.barrier()


def phase_attn(K, L):
    dr = K.dr
    smax = max(S for _, S in K.seqs)
    with ExitStack() as es:
        kT = K.sb(es, "t_kT", [128, smax], BF16)
        kpe = K.sb(es, "t_kpe", [64, smax], BF16)
        vv = K.sb(es, "t_v", [128, smax // 128, 128], BF16)
        qn = Rot([K.sb(es, "t_qn%d" % i, [128, TT], BF16) for i in range(2)])
        qp = Rot([K.sb(es, "t_qp%d" % i, [64, TT], BF16) for i in range(2)])
        pT = Rot([K.sb(es, "t_p%d" % i, [128, 2 * TT], BF16) for i in range(4)])
        onesf = K.sb(es, "t_onesf", [128, 128], F32)
        daccs = Rot([K.sb(es, "t_dacc%d" % i, [128, 2 * TT], F32) for i in range(2)])
        rden = Rot([K.sb(es, "t_rd%d" % i, [128, TT], F32) for i in range(2)])
        ob = Rot([K.sb(es, "t_ob%d" % i, [128, TT], BF16) for i in range(2)])
        K.sp.dma(onesf[:], dr.c["ones_f"], onesf, True)
        psS = Rot([K.psum2[0], K.psum2[1], K.psum2[3]])
        psOD = Rot([(K.psum[4], K.psum[5])])
        for (T0, S) in K.seqs:
            nj = S // 128
            K.sp.dma(kpe[:, 0:S], dr.kpeT[:, T0:T0 + S], kpe, True)
            for h in range(6):
                K.sp.dma(kT[:, 0:S], dr.kT[h, :, T0:T0 + S], kT, True)
                K.sp.dma(vv[:, 0:nj, :], dr.v_tok[T0:T0 + S, h * 128:(h + 1) * 128].rearrange("(j p) d -> p j d", p=128), vv, True)
                for qc in range(S // TT):
                    ta = T0 + qc * TT
                    q1 = qn.next(); q2 = qp.next()
                    K.sp.dma(q1[:], dr.qT[h, 0:128, ta:ta + TT], q1, True)
                    K.sp.dma(q2[:], dr.qT[h, 128:192, ta:ta + TT], q2, True)
                    psO, psD = psOD.next()
                    dacc = daccs.next()
                    sbanks = {}

                    def qk(jp):
                        ps = psS.next()
                        for hh in range(2):
                            j = 2 * jp + hh
                            K.pe.op(lambda e: e.matmul(ps[:, hh * 512:(hh + 1) * 512], kT[:, j * 128:(j + 1) * 128], q1[:, :], start=True, stop=False),
                                    reads=[kT, q1], writes=[ps])
                            K.pe.op(lambda e: e.matmul(ps[:, hh * 512:(hh + 1) * 512], kpe[0:64, j * 128:(j + 1) * 128], q2[0:64, :], start=False, stop=True),
                                    reads=[kpe, q2], writes=[ps])
                        sbanks[jp] = ps

                    def pv(jp):
                        ps = sbanks.pop(jp)
                        p = pT.next()
                        K.act.op(lambda e: e.activation(p[:, :], ps[:, :], AF.Exp), reads=[ps], writes=[p])
                        if jp == 0:
                            K.dve.op(lambda e: e.tensor_copy(dacc[:, :], p[:, :]), reads=[p], writes=[dacc])
                        else:
                            K.dve.op(lambda e: e.tensor_tensor(dacc[:, :], dacc[:, :], p[:, :], op=ALU.add), reads=[dacc, p], writes=[dacc])
                        for hh in range(2):
                            j = 2 * jp + hh
                            K.pe.op(lambda e: e.matmul(psO[:, :], vv[:, j, :], p[:, hh * 512:(hh + 1) * 512], start=(j == 0), stop=(j == nj - 1)),
                                    reads=[vv, p], writes=[psO])

                    LOOK = int(os.environ.get('ATT_LOOK', '2'))
                    npair = nj // 2
                    for jp in range(min(LOOK, npair)):
                        qk(jp)
                    for jp in range(npair):
                        if jp + LOOK < npair:
                            qk(jp + LOOK)
                        pv(jp)
                    K.pe.op(lambda e: e.matmul(psD[:, :], onesf[:, :], dacc[:, 0:512], start=True, stop=False), reads=[onesf, dacc], writes=[psD])
                    K.pe.op(lambda e: e.matmul(psD[:, :], onesf[:, :], dacc[:, 512:1024], start=False, stop=True), reads=[onesf, dacc], writes=[psD])
                    rd = rden.next(); o = ob.next()
                    K.dve.op(lambda e: e.reciprocal(rd[:, :], psD[:, :]), reads=[psD], writes=[rd])
                    K.dve.op(lambda e: e.tensor_tensor(o[:, :], psO[:, :], rd[:, :], op=ALU.mult), reads=[psO, rd], writes=[o])
                    K.pool.dma(dr.mixT[512 + h * 128:512 + (h + 1) * 128, ta:ta + TT], o[:, :], o, False)


def phase_w(K, L):
    dr = K.dr
    with ExitStack() as es:
        x1k = [K.sb(es, "w_x1_%d" % i, [128, TT], F32) for i in range(NKT)]
        acc = K.sb(es, "w_acc", [128, NKT, TT], F32)
        hb = K.sb(es, "w_hb", [128, NKT, TT], BF16)
        aT = K.sb(es, "w_aT", [128, 64, TT], BF16)
        sqs = Rot([K.sb(es, "w_sq%d" % i, [128, TT], BF16) for i in range(3)])
        wbs = Rot([K.sb(es, "w_wb%d" % i, [128, NKT, 512], BF16) for i in range(2)])
        rstd = K.sb(es, "w_rstd", [128, TT], F32)
        onesD = K.sb(es, "w_onesD", [128, 128], BF16)
        g1 = K.sb(es, "w_g1", [128, NKT], F32)
        g2 = K.sb(es, "w_g2", [128, NKT], F32)
        g3 = K.sb(es, "w_g3", [128, NKT], F32)
        tmp = Rot([K.sb(es, "w_tmp%d" % i, [128, TT], F32) for i in range(4)])
        K.sp.dma(onesD[:], dr.c["onesD"], onesD, True)
        with K.nc.allow_non_contiguous_dma("small gain vectors"):
            K.sp.dma(g1[:], dr.post_mix_g[L].rearrange("(kt p) -> p kt", p=128), g1, True)
            K.sp.dma(g2[:], dr.pre_ffn_g[L].rearrange("(kt p) -> p kt", p=128), g2, True)
            K.sp.dma(g3[:], dr.post_ffn_g[L].rearrange("(kt p) -> p kt", p=128), g3, True)
        psr = Rot(K.psum[1:8])
        ps_st = K.psum[0]

        ta_cur = [0]

        def post_norm_add(gv, final):
            rms_rep(K, ps_st, lambda kt: acc[:, kt, :], NKT, onesD, sqs, [acc], rstd)
            for kt in range(NKT):
                t = tmp.next()
                K.dve.op(lambda e: e.scalar_tensor_tensor(t[:, :], acc[:, kt, :], gv[:, kt:kt + 1], rstd[:], op0=ALU.mult, op1=ALU.mult),
                         reads=[acc, gv, rstd], writes=[t])
                eng = K.pool if kt % 4 == 3 else K.dve
                eng.op(lambda e: e.tensor_tensor(x1k[kt][:, :], x1k[kt][:, :], t[:, :], op=ALU.add), reads=[x1k[kt], t], writes=[x1k[kt]])
                if final:
                    K.pool.dma(dr.xT[kt * 128:(kt + 1) * 128, ta_cur[0]:ta_cur[0] + TT], x1k[kt][:, :], x1k[kt], False)

        for ti in range(K.T // TT):
            ta = ti * TT
            ta_cur[0] = ta
            for kt in range(NKT):
                K.sp.dma(x1k[kt][:, :], dr.xT[kt * 128:(kt + 1) * 128, ta:ta + TT], x1k[kt], True)
            K.sp.dma(hb[:], dr.mixT[:, ta:ta + TT].rearrange("(kt p) t -> p kt t", p=128), hb, True)
            for cg in range(4):
                wb = wbs.next()
                K.sp.dma(wb[:], dr.w_out_b[L, cg], wb, True)
                for m in range(4):
                    ps = psr.next()
                    for kt in range(NKT):
                        K.pe.op(lambda e: e.matmul(ps[:, :], wb[:, kt, m * 128:(m + 1) * 128], hb[:, kt, :], start=(kt == 0), stop=(kt == NKT - 1)),
                                reads=[wb, hb], writes=[ps])
                    evac(K, acc[:, cg * 4 + m, :], acc, ps, ps[:, :])
            post_norm_add(g1, False)
            rms_rep(K, ps_st, lambda kt: x1k[kt][:, :], NKT, onesD, sqs, x1k, rstd)
            for kt in range(NKT):
                K.dve.op(lambda e: e.scalar_tensor_tensor(hb[:, kt, :], x1k[kt][:, :], g2[:, kt:kt + 1], rstd[:], op0=ALU.mult, op1=ALU.mult),
                         reads=[x1k[kt], g2, rstd], writes=[hb])
            for cg in range(16):
                wb = wbs.next()
                K.sp.dma(wb[:], dr.w_ff1_b[L, cg], wb, True)
                for m in range(4):
                    ps = psr.next()
                    for kt in range(NKT):
                        K.pe.op(lambda e: e.matmul(ps[:, :], wb[:, kt, m * 128:(m + 1) * 128], hb[:, kt, :], start=(kt == 0), stop=(kt == NKT - 1)),
                                reads=[wb, hb], writes=[ps])
                    t = tmp.next()
                    K.act.op(lambda e: e.activation(t[:, :], ps[:, :], AF.Relu), reads=[ps], writes=[t])
                    K.dve.op(lambda e: e.tensor_tensor(aT[:, cg * 4 + m, :], t[:, :], t[:, :], op=ALU.mult), reads=[t], writes=[aT])
            for cg in range(4):
                pss = [psr.next() for _ in range(4)]
                for kb in range(4):
                    wb = wbs.next()
                    K.sp.dma(wb[:], dr.w_ff2_b[L, cg * 4 + kb], wb, True)
                    for m in range(4):
                        for kt in range(NKT):
                            K.pe.op(lambda e: e.matmul(pss[m][:, :], wb[:, kt, m * 128:(m + 1) * 128], aT[:, kb * 16 + kt, :],
                                                       start=(kb == 0 and kt == 0), stop=(kb == 3 and kt == NKT - 1)),
                                    reads=[wb, aT], writes=[pss[m]])
                for m in range(4):
                    evac(K, acc[:, cg * 4 + m, :], acc, pss[m], pss[m][:, :])
            post_norm_add(g3, True)


N_CORES = 8
SEQ_LENS = (16384, 2048, 2048)
DEPTH = 2
_CACHE = {}


def kernel(x_prompt, x_sample, pre_mix_g, w_in, q_norm_g, w_q_up, kv_norm_g, w_kv_up,
           conv_w, conv_b, dt_bias_f, dt_bias_b, a_log_f, a_log_b, d_skip, ssd_norm_g,
           w_out, post_mix_g, pre_ffn_g, w_ff1, w_ff2, post_ffn_g):
    f = lambda a: np.ascontiguousarray(np.asarray(a, dtype=np.float32))
    x_prompt = f(x_prompt)
    x_sample = f(x_sample)
    if "nc" not in _CACHE:
        _CACHE["nc"] = build_program(list(SEQ_LENS), DEPTH)
    nc, consts = _CACHE["nc"]
    shared = {
        "pre_mix_g": f(pre_mix_g), "w_in": f(w_in), "q_norm_g": f(q_norm_g), "w_q_up": f(w_q_up),
        "kv_norm_g": f(kv_norm_g), "w_kv_up": f(w_kv_up), "conv_w": f(conv_w), "conv_b": f(conv_b),
        "dt_bias": np.ascontiguousarray(np.concatenate([f(dt_bias_f), f(dt_bias_b)], axis=1)),
        "a_log": np.ascontiguousarray(np.concatenate([f(a_log_f), f(a_log_b)], axis=1)),
        "d_skip": f(d_skip), "ssd_norm_g": f(ssd_norm_g), "w_out": f(w_out), "post_mix_g": f(post_mix_g),
        "pre_ffn_g": f(pre_ffn_g), "w_ff1": f(w_ff1), "w_ff2": f(w_ff2), "post_ffn_g": f(post_ffn_g),
    }
    for k, v in consts.items():
        shared["c_" + k] = v
    in_maps = []
    for c in range(N_CORES):
        xp = x_prompt[c] if c < x_prompt.shape[0] else np.zeros_like(x_prompt[0])
        x = np.concatenate([xp, x_sample[2 * c], x_sample[2 * c + 1]], axis=0)
        m = dict(shared)
        m["x"] = np.ascontiguousarray(x)
        in_maps.append(m)
    res = run_bass_kernel_spmd(nc, in_maps, core_ids=list(range(N_CORES)))
    ys = [np.asarray(r["y"]) for r in res.results]
    SP = SEQ_LENS[0]
    SS = SEQ_LENS[1]
    y_prompt = np.stack([ys[c][0:SP] for c in range(x_prompt.shape[0])], axis=0).astype(np.float32)
    y_sample = np.empty_like(x_sample)
    for c in range(N_CORES):
        y_sample[2 * c] = ys[c][SP:SP + SS]
        y_sample[2 * c + 1] = ys[c][SP + SS:SP + 2 * SS]
    return (y_prompt, y_sample)
```
